# Optimizing a Trainium2 kernel written in Bass

```python
import jax, jax.numpy as jnp
from jax import lax
import numpy as np

D_MODEL = 1024
BATCH = 16
SEQ = 4096
DEPTH = 1

ATTN_HEADS = 8
KV_HEADS = 2
HEAD_DIM = 64
ATTN_WIDTH = ATTN_HEADS * HEAD_DIM
KV_WIDTH = KV_HEADS * HEAD_DIM
GROUP = ATTN_HEADS // KV_HEADS
WINDOW = 128
BLOCK = 128
CONV_WIDTH = D_MODEL - ATTN_WIDTH
CONV_K = 3
D_FF = 4 * D_MODEL
EPS = 1e-5
IN_WIDTH = ATTN_WIDTH + 2 * KV_WIDTH + 3 * CONV_WIDTH
SPLITS = (ATTN_WIDTH,
          ATTN_WIDTH + KV_WIDTH,
          ATTN_WIDTH + 2 * KV_WIDTH,
          ATTN_WIDTH + 2 * KV_WIDTH + CONV_WIDTH,
          ATTN_WIDTH + 2 * KV_WIDTH + 2 * CONV_WIDTH)

kernel_name = "hymba_swa_sink_alibi_shortconv_relu2"


def _rmsnorm(x, g):
    xf = x.astype(jnp.float32)
    inv = lax.rsqrt(jnp.mean(xf * xf, axis=-1, keepdims=True) + EPS)
    return (xf * inv * g.astype(jnp.float32)).astype(x.dtype)


def _alibi_slopes(n):
    return jnp.exp2(-8.0 * (jnp.arange(n, dtype=jnp.float32) + 1.0) / n)


def _window_attention(q, k, v, sinks):
    b, s, _ = q.shape
    nb = s // BLOCK
    qb = q.reshape(b, nb, BLOCK, KV_HEADS, GROUP, HEAD_DIM)
    kb = k.reshape(b, nb, BLOCK, KV_HEADS, HEAD_DIM)
    vb = v.reshape(b, nb, BLOCK, KV_HEADS, HEAD_DIM)
    pad = ((0, 0), (1, 0), (0, 0), (0, 0), (0, 0))
    k_band = jnp.concatenate([jnp.pad(kb, pad)[:, :-1], kb], axis=2)
    v_band = jnp.concatenate([jnp.pad(vb, pad)[:, :-1], vb], axis=2)

    scale = HEAD_DIM ** -0.5
    scores = jnp.einsum('bnqkgd,bnskd->bkgnqs', qb, k_band,
                        preferred_element_type=jnp.float32) * scale

    qi = jnp.arange(BLOCK)[:, None]
    kj = jnp.arange(2 * BLOCK)[None, :]
    dist = qi - kj + BLOCK
    blk = jnp.arange(nb)[:, None, None]
    key_pos = blk * BLOCK - BLOCK + kj[None]
    mask = (dist[None] >= 0) & (dist[None] < WINDOW) & (key_pos >= 0)

    slopes = _alibi_slopes(ATTN_HEADS).reshape(KV_HEADS, GROUP)
    bias = -slopes[:, :, None, None, None] * dist.astype(jnp.float32)
    logits = jnp.where(mask, scores + bias, -jnp.inf)

    sink = sinks.astype(jnp.float32).reshape(KV_HEADS, GROUP)[None, :, :, None, None, None]
    m = jnp.maximum(jnp.max(logits, axis=-1, keepdims=True), sink)
    p = jnp.exp(logits - m)
    denom = jnp.sum(p, axis=-1, keepdims=True) + jnp.exp(sink - m)
    probs = (p / denom).astype(v.dtype)
    out = jnp.einsum('bkgnqs,bnskd->bnqkgd', probs, v_band)
    return out.reshape(b, s, ATTN_WIDTH)


def _short_conv(u, w):
    c = u.shape[-1]
    return lax.conv_general_dilated(
        u, w[:, None, :].astype(u.dtype), window_strides=(1,),
        padding=[(CONV_K - 1, 0)], dimension_numbers=('NWC', 'WIO', 'NWC'),
        feature_group_count=c)


def setup_inputs(seed: int = 0) -> dict:
    key = jax.random.key(seed)
    ks = jax.random.split(key, 13)
    f32 = jnp.float32

    def gain(k, n):
        return 1.0 + 0.02 * jax.random.normal(k, (DEPTH, n), f32)

    x = jax.random.normal(ks[0], (BATCH, SEQ, D_MODEL), f32)
    norm1_g = gain(ks[1], D_MODEL)
    w_in = jax.random.normal(ks[2], (DEPTH, D_MODEL, IN_WIDTH), f32) * D_MODEL ** -0.5
    conv_w = jax.random.normal(ks[3], (DEPTH, CONV_K, CONV_WIDTH), f32) * CONV_K ** -0.5
    sinks = jax.random.normal(ks[4], (DEPTH, ATTN_HEADS), f32)
    attn_norm_g = gain(ks[5], ATTN_WIDTH)
    conv_norm_g = gain(ks[6], CONV_WIDTH)
    w_out = jax.random.normal(ks[7], (DEPTH, D_MODEL, D_MODEL), f32) * D_MODEL ** -0.5
    norm2_g = gain(ks[8], D_MODEL)
    w_ff1 = jax.random.normal(ks[9], (DEPTH, D_MODEL, D_FF), f32) * D_MODEL ** -0.5
    w_ff2 = jax.random.normal(ks[10], (DEPTH, D_FF, D_MODEL), f32) * D_FF ** -0.5
    final_g = 1.0 + 0.02 * jax.random.normal(ks[11], (D_MODEL,), f32)
    return {"x": x, "norm1_g": norm1_g, "w_in": w_in, "conv_w": conv_w,
            "sinks": sinks, "attn_norm_g": attn_norm_g, "conv_norm_g": conv_norm_g,
            "w_out": w_out, "norm2_g": norm2_g, "w_ff1": w_ff1, "w_ff2": w_ff2,
            "final_g": final_g}


def reference(x, norm1_g, w_in, conv_w, sinks, attn_norm_g, conv_norm_g,
              w_out, norm2_g, w_ff1, w_ff2, final_g):
    h = x
    for l in range(DEPTH):
        y = _rmsnorm(h, norm1_g[l])
        proj = jnp.einsum('bsd,de->bse', y, w_in[l])
        q, k, v, c_gate, b_gate, u = jnp.split(proj, SPLITS, axis=-1)
        attn = _window_attention(q, k, v, sinks[l])
        conv = b_gate * _short_conv(c_gate * u, conv_w[l])
        mixed = jnp.concatenate([_rmsnorm(attn, attn_norm_g[l]),
                                 _rmsnorm(conv, conv_norm_g[l])], axis=-1)
        h = h + jnp.einsum('bse,ed->bsd', mixed, w_out[l])
        z = _rmsnorm(h, norm2_g[l])
        a = jnp.square(jax.nn.relu(jnp.einsum('bsd,df->bsf', z, w_ff1[l])))
        h = h + jnp.einsum('bsf,fd->bsd', a, w_ff2[l])
    return _rmsnorm(h, final_g)
```

```python
import numpy as np
from contextlib import ExitStack
import concourse.bass as bass
import concourse.mybir as mybir
from concourse.bass_utils import run_bass_kernel_spmd

F32 = mybir.dt.float32
BF16 = mybir.dt.bfloat16
AF = mybir.ActivationFunctionType
ALU = mybir.AluOpType

NCORES = 8
D = 1024
SEQ = 4096
T = 512
NT = 16
TPS = 8
EPS = 1e-5
NS = 5
CH_Q, CH_KV, CH_V0, CH_F0, CH_G0, CH_H0 = 0, 1, 2, 6, 8, 16
NCHUNK = 24
CH_SIZE = [4096] * NCHUNK
CH_SIZE[CH_KV] = 2048
for _i in range(4):
    CH_SIZE[CH_V0 + _i] = 3072


class Tok:
    __slots__ = ("sem", "val", "eng")

    def __init__(self, sem, val, eng):
        self.sem, self.val, self.eng = sem, val, eng


class Buf:
    def __init__(self, name, t=None):
        self.name, self.t = name, t
        self.w = None
        self.r = {}


class Eng:
    def __init__(self, nc, h, name, es, is_pe=False):
        self.h = h
        self.name = name
        self.sem = es.enter_context(nc.semaphore("e_" + name))
        self.count = 0
        self.waited = {}
        self.is_pe = is_pe


class DSem:
    def __init__(self, nc, name, es):
        self.sem = es.enter_context(nc.semaphore("d_" + name))
        self.count = 0


class Dep:
    def wait(self, eng, reads, writes):
        need = {}

        def add(tok, raw):
            if tok is None:
                return
            if tok.eng is eng:
                if eng.is_pe or not raw:
                    return
            k = id(tok.sem)
            if k not in need or need[k].val < tok.val:
                need[k] = tok

        for b in reads:
            add(b.w, True)
        for b in writes:
            add(b.w, False)
            for t in b.r.values():
                add(t, False)
        for k, tok in need.items():
            if eng.waited.get(k, 0) < tok.val:
                eng.h.wait_ge(tok.sem, tok.val)
                eng.waited[k] = tok.val

    def _update(self, tok, reads, writes):
        for b in writes:
            b.w = tok
            b.r = {}
        for b in reads:
            b.r[id(tok.sem)] = tok

    dry = False

    def op(self, eng, fn, reads=(), writes=()):
        if self.dry:
            return
        self.wait(eng, reads, writes)
        inst = fn()
        eng.count += 1
        inst.then_inc(eng.sem, 1)
        self._update(Tok(eng.sem, eng.count, eng), reads, writes)

    def group(self, eng, fns, reads=(), writes=()):
        if self.dry:
            return
        self.wait(eng, reads, writes)
        inst = None
        for fn in fns:
            inst = fn()
        eng.count += 1
        inst.then_inc(eng.sem, 1)
        self._update(Tok(eng.sem, eng.count, eng), reads, writes)

    def dma(self, qeng, fns, dsem, reads=(), writes=()):
        if self.dry:
            return
        self.wait(qeng, reads, writes)
        for fn in fns:
            fn().then_inc(dsem.sem, 16)
            dsem.count += 16
        self._update(Tok(dsem.sem, dsem.count, None), reads, writes)


def build(nt=NT, do_prologue=True):
    nc = bass.Bass("TRN2", target_bir_lowering=False)

    def din(name, shape):
        return nc.dram_tensor(name, shape, F32, kind="ExternalInput").ap()

    x_d = din("x", [2 * SEQ, D])
    w_in_d = din("w_in", [D, 2304])
    w_out_d = din("w_out", [D, D])
    w_ff1_d = din("w_ff1", [D, 4096])
    w_ff2_d = din("w_ff2", [4096, D])
    g1_d = din("g1", [128, D])
    g2_d = din("g2", [128, D])
    gf_d = din("gf", [128, D])
    ga_d = din("ga", [128, 512])
    gc_d = din("gc", [128, 4])
    cw_d = din("cw", [128, 12])
    sk_d = din("sk", [128, 8])
    id_d = din("ident", [128, 128])
    bias_d = din("bias", [128, 2048])
    out_d = nc.dram_tensor("out", [2 * SEQ, D], F32, kind="ExternalOutput").ap()
    wscr = nc.dram_tensor("wscr", [NCHUNK, 128, 4096], BF16).ap()

    dep = Dep()
    with ExitStack() as es:
        def sb(name, shape, dt):
            return Buf(name, es.enter_context(nc.sbuf_tensor("s_" + name, shape, dt)))

        def ps(name, shape, dt):
            return Buf(name, es.enter_context(nc.psum_tensor("p_" + name, shape, dt)))

        PE = Eng(nc, nc.tensor, "pe", es, is_pe=True)
        ACT = Eng(nc, nc.scalar, "act", es)
        DVE = Eng(nc, nc.vector, "dve", es)
        POOL = Eng(nc, nc.gpsimd, "pool", es)
        SP = Eng(nc, nc.sync, "sp", es)

        X = [sb(f"X{i}", [128, 4, D], F32) for i in range(2)]
        junk = es.enter_context(nc.sbuf_tensor("junk", [128, D], BF16))
        xsB = sb("xsB", [128, D], BF16)
        xsA = sb("xsA", [128, D], BF16)
        xT = sb("xT", [128, 8, T], BF16)
        zT = sb("zT", [128, 8, T], BF16)
        qz = [sb(f"qz{i}", [128, 4, T], BF16) for i in range(2)]
        kT = [sb(f"kT{i}", [128, T], BF16) for i in range(2)]
        vtok = [sb(f"vtok{i}", [128, 4, 2, 65], BF16) for i in range(2)]
        u_sb = sb("u_sb", [128, T], F32)
        cu = sb("cu", [128, 4, T + 2], F32)
        conv_f = sb("conv_f", [128, 4, T], F32)
        sq = sb("sq", [128, T], BF16)
        invc = sb("invc", [128, T], F32)
        mixedT = sb("mixedT", [128, 8, T], BF16)
        PT = sb("PT", [128, 4, 512], BF16)
        xin = [sb(f"xin{i}", [128, D], F32) for i in range(2)]
        attn_n = sb("attn_n", [128, 512], F32)
        attn_s = sb("attn_s", [128, 512], BF16)
        aT = sb("aT", [128, 32, T], BF16)
        relu_t = [sb(f"relu{i}", [128, T], F32) for i in range(2)]
        R = [sb(f"ring{i}", [128, 4096], BF16) for i in range(NS)]
        ident = sb("ident", [128, 128], BF16)
        ones_bf = sb("ones_bf", [128, 128], BF16)
        biasT = sb("biasT", [128, 4, 512], BF16)
        g1 = sb("g1", [128, D], F32)
        g2 = sb("g2", [128, D], F32)
        gf = sb("gf", [128, D], F32)
        ga = sb("ga", [128, 512], F32)
        gc = sb("gc", [128, 4], F32)
        cw = sb("cw", [128, 12], F32)
        es_bc = sb("es_bc", [128, 8], F32)
        epsc = sb("epsc", [128, 1], F32)
        stats = {}
        for key in ("A0", "B0", "B1", "Bq", "Af"):
            stats[key] = [sb(f"st_{key}{i}", [128, 4], F32) for i in range(3)]
        den = sb("den", [128, 8], F32)
        rden = sb("rden", [128, 8], F32)

        FA = [ps(f"FA{i}", [128, 512], F32) for i in range(4)]
        PB = [ps(f"PB{i}", [128, 512], F32) for i in range(4)]

        def tpview(bank):
            return bank.t[:, :].bitcast(BF16).rearrange("p (k j) -> p k j", j=128)

        Lx = [DSem(nc, f"lx{i}", es) for i in range(2)]
        Lin = [DSem(nc, f"lin{i}", es) for i in range(2)]
        Sx = [DSem(nc, f"sx{i}", es) for i in range(2)]
        RS = [DSem(nc, f"rs{i}", es) for i in range(NS)]
        RST = [DSem(nc, f"rst{i}", es) for i in range(NS)]
        CS = DSem(nc, "cs", es)
        WS = [Buf(f"ws{c}") for c in range(NCHUNK)]

        cstage = X[1]
        cst = cstage.t[:].rearrange("p a b -> p (a b)")
        dep.dma(SP, [
            lambda: nc.sync.dma_start(out=cst[:, 0:128], in_=id_d[:, :]),
            lambda: nc.sync.dma_start(out=cst[:, 128:128 + 2048], in_=bias_d[:, :]),
        ], Lx[1], writes=[cstage])
        dep.dma(SP, [
            lambda: nc.sync.dma_start(out=g1.t[:, :], in_=g1_d[:, :]),
            lambda: nc.sync.dma_start(out=g2.t[:, :], in_=g2_d[:, :]),
            lambda: nc.sync.dma_start(out=gf.t[:, :], in_=gf_d[:, :]),
            lambda: nc.sync.dma_start(out=ga.t[:, :], in_=ga_d[:, :]),
            lambda: nc.sync.dma_start(out=gc.t[:, :], in_=gc_d[:, :]),
            lambda: nc.sync.dma_start(out=cw.t[:, :], in_=cw_d[:, :]),
            lambda: nc.sync.dma_start(out=es_bc.t[:, :], in_=sk_d[:, :]),
        ], CS, writes=[g1, g2, gf, ga, gc, cw, es_bc])
        dep.op(DVE, lambda: nc.vector.tensor_copy(out=ident.t[:, :], in_=cst[:, 0:128]),
               reads=[cstage], writes=[ident])
        dep.op(DVE, lambda: nc.vector.tensor_copy(
            out=biasT.t[:].rearrange("p a b -> p (a b)"), in_=cst[:, 128:128 + 2048]),
            reads=[cstage], writes=[biasT])
        dep.op(DVE, lambda: nc.vector.memset(ones_bf.t[:, :], 1.0), writes=[ones_bf])
        dep.op(DVE, lambda: nc.vector.memset(epsc.t[:, :], EPS), writes=[epsc])
        for i in range(2):
            dep.op(DVE, lambda i=i: nc.vector.memset(
                vtok[i].t[:].rearrange("p a b c -> p (a b c)"), 1.0), writes=[vtok[i]])
            dep.op(DVE, lambda i=i: nc.vector.memset(
                qz[i].t[:].rearrange("p a b -> p (a b)"), 0.0), writes=[qz[i]])
        dep.op(ACT, lambda: nc.scalar.activation(out=es_bc.t[:, :], in_=es_bc.t[:, :], func=AF.Exp),
               reads=[es_bc], writes=[es_bc])

        def src_views(c):
            res = []
            wv_in = w_in_d.rearrange("(k p) c -> p k c", p=128)
            if c == CH_Q:
                for j in range(4):
                    for half in range(2):
                        col = (half * 4 + j) * 64
                        res.append((lambda s, j=j, half=half: s.rearrange("p (k c) -> p k c", c=512)[
                            :, :, j * 128 + half * 64: j * 128 + half * 64 + 64],
                            wv_in[:, :, col:col + 64]))
            elif c == CH_KV:
                res.append((lambda s: s[:, 0:2048].rearrange("p (k c) -> p k c", c=256),
                            wv_in[:, :, 512:768]))
            elif CH_V0 <= c < CH_V0 + 4:
                ct = c - CH_V0
                for ui, c0 in enumerate((1792, 768, 1280)):
                    res.append((lambda s, ui=ui: s[:, 0:3072].rearrange("p (k c) -> p k c", c=384)[
                        :, :, ui * 128:(ui + 1) * 128],
                        wv_in[:, :, c0 + ct * 128:c0 + (ct + 1) * 128]))
            elif c in (CH_F0, CH_F0 + 1):
                dh = c - CH_F0
                wv = w_out_d.rearrange("(e p) d -> p e d", p=128)
                res.append((lambda s: s.rearrange("p (e d) -> p e d", d=512), wv[:, :, dh * 512:(dh + 1) * 512]))
            elif CH_G0 <= c < CH_H0:
                j = c - CH_G0
                wv = w_ff1_d.rearrange("(k p) f -> p k f", p=128)
                res.append((lambda s: s.rearrange("p (k c) -> p k c", c=512), wv[:, :, j * 512:(j + 1) * 512]))
            else:
                j = c - CH_H0
                dh, fg = j // 4, j % 4
                wv = w_ff2_d.rearrange("(f p) d -> p f d", p=128)
                res.append((lambda s: s.rearrange("p (k c) -> p k c", c=512),
                            wv[:, fg * 8:(fg + 1) * 8, dh * 512:(dh + 1) * 512]))
            return res

        PCS = [DSem(nc, f"pc{c}", es) for c in range(NCHUNK)]

        pro_state = {"next": 0}

        def ensure_prologue(cmax):
            if not do_prologue:
                return
            while pro_state["next"] <= cmax:
                c = pro_state["next"]
                pro_state["next"] += 1
                n = CH_SIZE[c]
                fns = []
                for (dv, src) in src_views(c):
                    fns.append(lambda dv=dv, src=src, c=c, n=n: nc.gpsimd.dma_start(
                        out=dv(wscr[c, :, 0:n]) if n == 4096 else dv(wscr[c]), in_=src))
                dep.dma(POOL, fns, PCS[c], writes=[WS[c]])

        class Stream:
            def __init__(self, order=None):
                self.record = order is None
                self.order = [] if order is None else order
                self.acq = 0
                self.issued = 0
                self.free = list(range(NS))
                self.slot_of = {}
                if not self.record:
                    self.pump(limit=2)

            def pump(self, limit=None):
                while self.issued < len(self.order) and self.free and (limit is None or self.issued < limit):
                    n = self.issued
                    c = self.order[n]
                    slot = self.free.pop(0)
                    self.slot_of[n] = slot
                    sz = CH_SIZE[c]
                    ensure_prologue(c)
                    dep.dma(SP, [lambda: nc.sync.dma_start(out=R[slot].t[:, 0:sz], in_=wscr[c, :, 0:sz])],
                            RS[slot], reads=[WS[c]], writes=[R[slot]])
                    self.issued += 1

            def acquire(self, c):
                n = self.acq
                self.acq += 1
                if self.record:
                    self.order.append(c)
                    return n, R[0]
                self.pump()
                assert self.order[n] == c, (n, c, self.order[n])
                assert n < self.issued, ("ring too small / order deadlock", n, self.issued)
                return n, R[self.slot_of[n]]

            def release(self, n):
                if self.record:
                    return
                self.free.append(self.slot_of[n])
                self.pump()

        def emit_all(stream):
            BLOCKED = ("blocked",)
            flags = {"mixer_done": set(), "store_done": set(), "x_loaded": set()}

            def load_xin(t, g):
                seq, tb = t // TPS, t % TPS
                r0 = seq * SEQ + tb * T + g * 128
                b = xin[g % 2]
                dep.dma(POOL, [lambda: nc.gpsimd.dma_start(out=b.t[:, :], in_=x_d[r0:r0 + 128, :])],
                        Lin[g % 2], writes=[b])

            def load_x(t):
                seq, tb = t // TPS, t % TPS
                r0 = seq * SEQ + tb * T
                xb = X[t % 2]
                dep.dma(POOL, [lambda: nc.gpsimd.dma_start(
                    out=xb.t[:, :, :], in_=x_d[r0:r0 + T, :].rearrange("(g p) d -> p g d", p=128))],
                    Lx[t % 2], writes=[xb])
                flags["x_loaded"].add(t)

            def try_load_x(t):
                if t in flags["x_loaded"]:
                    return True
                if t >= 2 and (t - 2) not in flags["store_done"]:
                    return False
                load_x(t)
                return True

            def store_x(t):
                seq, tb = t // TPS, t % TPS
                r0 = seq * SEQ + tb * T
                xb = X[t % 2]
                dep.dma(POOL, [lambda: nc.gpsimd.dma_start(
                    out=out_d[r0:r0 + T, :].rearrange("(g p) d -> p g d", p=128), in_=xb.t[:, :, :])],
                    Sx[t % 2], reads=[xb])
                flags["store_done"].add(t)

            def rms_inv_batch(srcs, n_feat, key):
                ss, ln_, inv = stats[key]
                nb_ = len(srcs)
                for i, (sap, sbuf) in enumerate(srcs):
                    dep.op(ACT, lambda: nc.scalar.activation(
                        out=junk[:, 0:n_feat], in_=sap, func=AF.Square, accum_out=ss.t[:, i:i + 1]),
                        reads=[sbuf], writes=[ss])
                dep.op(ACT, lambda: nc.scalar.activation(
                    out=ln_.t[:, 0:nb_], in_=ss.t[:, 0:nb_], func=AF.Ln, scale=1.0 / n_feat, bias=epsc.t[:, 0:1]),
                    reads=[ss, epsc], writes=[ln_])
                dep.op(ACT, lambda: nc.scalar.activation(
                    out=inv.t[:, 0:nb_], in_=ln_.t[:, 0:nb_], func=AF.Exp, scale=-0.5),
                    reads=[ln_], writes=[inv])
                return inv

            def norm_transpose(src, gtab, dstT, xsb, bank, th, batch, after_stt=None):
                tpv = tpview(bank)
                for bi, g0 in enumerate(range(0, 4, batch)):
                    gs = list(range(g0, g0 + batch))
                    inv = rms_inv_batch([src(g) for g in gs], D, th + str(bi))
                    yield (1.2 * batch + 0.6, 0)
                    for i, g in enumerate(gs):
                        sap, sbuf = src(g)
                        dep.op(DVE, lambda: nc.vector.scalar_tensor_tensor(
                            out=xsb.t[:, :], in0=sap, scalar=inv.t[:, i:i + 1], in1=gtab.t[:, :],
                            op0=ALU.mult, op1=ALU.mult), reads=[sbuf, inv, gtab], writes=[xsb])
                        if after_stt is not None:
                            after_stt(g)
                        yield (1.3, 0)
                        dep.group(PE, [lambda k=k: nc.tensor.transpose(
                            out=tpv[:, k, :], in_=xsb.t[:, k * 128:(k + 1) * 128], identity=ident.t[:, :])
                            for k in range(8)], reads=[xsb, ident], writes=[bank])
                        dep.op(ACT, lambda: nc.scalar.activation(
                            out=dstT.t[:, :, g * 128:(g + 1) * 128], in_=tpv[:, :, :], func=AF.Copy),
                            reads=[bank], writes=[dstT])
                        yield (1.7, 0)

            def proj_unit(bank, wslot, wview, col0, srcT):
                dep.group(PE, [lambda k=k: nc.tensor.matmul(
                    bank.t[:, :], lhsT=wview[:, k, col0:col0 + 128], rhs=srcT.t[:, k, :],
                    start=(k == 0), stop=(k == 7)) for k in range(8)],
                    reads=[wslot, srcT], writes=[bank])

            def mixer(t):
                seq, tb = t // TPS, t % TPS
                xb = X[t % 2]
                par = t % 2
                try_load_x(t)

                def after_stt(g):
                    if g + 2 < 4:
                        load_xin(t, g + 2)
                    elif t + 1 < nt:
                        load_xin(t + 1, g - 2)

                yield from norm_transpose(lambda g: (xin[g % 2].t[:, :], xin[g % 2]), g1, xT, xsB, PB[0], "B", 2,
                                          after_stt)
                n, Qc = stream.acquire(CH_Q)
                Qv = Qc.t[:].rearrange("p (k c) -> p k c", c=512)
                for j in range(4):
                    bank = PB[1 + (j % 2)]
                    proj_unit(bank, Qc, Qv, j * 128, xT)
                    dep.op(ACT, lambda: nc.scalar.activation(
                        out=qz[0].t[0:64, j, :], in_=bank.t[0:64, :], func=AF.Copy),
                        reads=[bank], writes=[qz[0]])
                    dep.op(DVE, lambda: nc.vector.tensor_copy(
                        out=qz[1].t[64:128, j, :], in_=bank.t[64:128, :]), reads=[bank], writes=[qz[1]])
                    yield (2.0, 0)
                stream.release(n)
                n, KVc = stream.acquire(CH_KV)
                KVv = KVc.t[:, 0:2048].rearrange("p (k c) -> p k c", c=256)
                bank = PB[1]
                proj_unit(bank, KVc, KVv, 0, xT)
                dep.op(DVE, lambda: nc.vector.tensor_copy(out=kT[par].t[:, :], in_=bank.t[:, :]),
                       reads=[bank], writes=[kT[par]])
                yield (2.0, 0)
                bank = PB[2]
                fns = []
                for tg in range(4):
                    for k in range(8):
                        fns.append(lambda tg=tg, k=k: nc.tensor.matmul(
                            bank.t[:, tg * 128:(tg + 1) * 128], lhsT=xT.t[:, k, tg * 128:(tg + 1) * 128],
                            rhs=KVv[:, k, 128:256], start=(k == 0), stop=(k == 7)))
                dep.group(PE, fns, reads=[KVc, xT], writes=[bank])
                dep.op(DVE, lambda: nc.vector.tensor_copy(
                    out=vtok[par].t[:, :, :, 0:64],
                    in_=bank.t[:, :].rearrange("p (a b c) -> p a b c", a=4, b=2)),
                    reads=[bank], writes=[vtok[par]])
                stream.release(n)
                yield (3.0, 0)
                if tb == 0:
                    dep.op(DVE, lambda: nc.vector.memset(cu.t[:, :, 0:2], 0.0), writes=[cu])
                ssc = PB[3]
                for ct in range(4):
                    n, Vc = stream.acquire(CH_V0 + ct)
                    Vv = Vc.t[:, 0:3072].rearrange("p (k c) -> p k c", c=384)
                    bank = PB[1]
                    proj_unit(bank, Vc, Vv, 0, xT)
                    dep.op(ACT, lambda: nc.scalar.activation(out=u_sb.t[:, :], in_=bank.t[:, :], func=AF.Copy),
                           reads=[bank], writes=[u_sb])
                    yield (2.0, 0)
                    bank = PB[2]
                    proj_unit(bank, Vc, Vv, 128, xT)
                    dep.op(DVE, lambda: nc.vector.tensor_tensor(
                        out=cu.t[:, ct, 2:T + 2], in0=bank.t[:, :], in1=u_sb.t[:, :], op=ALU.mult),
                        reads=[bank, u_sb], writes=[cu])
                    y = conv_f.t[:, ct, :]
                    dep.op(DVE, lambda: nc.vector.tensor_scalar(
                        out=y, in0=cu.t[:, ct, 2:T + 2], scalar1=cw.t[:, ct * 3 + 2:ct * 3 + 3], scalar2=None,
                        op0=ALU.mult), reads=[cu, cw], writes=[conv_f])
                    dep.op(DVE, lambda: nc.vector.scalar_tensor_tensor(
                        out=y, in0=cu.t[:, ct, 1:T + 1], scalar=cw.t[:, ct * 3 + 1:ct * 3 + 2], in1=y,
                        op0=ALU.mult, op1=ALU.add), reads=[cu, cw, conv_f], writes=[conv_f])
                    dep.op(DVE, lambda: nc.vector.scalar_tensor_tensor(
                        out=y, in0=cu.t[:, ct, 0:T], scalar=cw.t[:, ct * 3:ct * 3 + 1], in1=y,
                        op0=ALU.mult, op1=ALU.add), reads=[cu, cw, conv_f], writes=[conv_f])
                    yield (4.0, 0)
                    bank = PB[1]
                    proj_unit(bank, Vc, Vv, 256, xT)
                    stream.release(n)
                    dep.op(DVE, lambda: nc.vector.tensor_tensor(
                        out=y, in0=bank.t[:, :], in1=y, op=ALU.mult), reads=[bank, conv_f], writes=[conv_f])
                    dep.op(ACT, lambda: nc.scalar.activation(out=sq.t[:, :], in_=y, func=AF.Square),
                           reads=[conv_f], writes=[sq])
                    yield (3.3, 0)
                    dep.group(PE, [lambda: nc.tensor.matmul(
                        ssc.t[:, :], lhsT=ones_bf.t[:, :], rhs=sq.t[:, :], start=(ct == 0), stop=(ct == 3))],
                        reads=[sq, ones_bf], writes=[ssc])
                    yield (0.3, 0)
                if tb != TPS - 1:
                    dep.op(DVE, lambda: nc.vector.tensor_copy(out=cu.t[:, :, 0:2], in_=cu.t[:, :, T:T + 2]),
                           reads=[cu], writes=[cu])
                dep.op(ACT, lambda: nc.scalar.activation(
                    out=invc.t[:, :], in_=ssc.t[:, :], func=AF.Ln, scale=1.0 / 512, bias=epsc.t[:, 0:1]),
                    reads=[ssc, epsc], writes=[invc])
                dep.op(ACT, lambda: nc.scalar.activation(
                    out=invc.t[:, :], in_=invc.t[:, :], func=AF.Exp, scale=-0.5), reads=[invc], writes=[invc])
                for ct in range(4):
                    dep.op(DVE, lambda: nc.vector.scalar_tensor_tensor(
                        out=mixedT.t[:, 4 + ct, :], in0=conv_f.t[:, ct, :], scalar=gc.t[:, ct:ct + 1],
                        in1=invc.t[:, :], op0=ALU.mult, op1=ALU.mult),
                        reads=[conv_f, gc, invc], writes=[mixedT])
                try_load_x(t)
                yield (2.0, 0)

                for qb in range(4):
                    nb = tb * 4 + qb
                    blks = [1] if nb == 0 else [0, 1]
                    pt = PT
                    for g in range(2):
                        for blk in blks:
                            if blk == 1:
                                ksrc, kc0 = kT[par], qb * 128
                            elif qb > 0:
                                ksrc, kc0 = kT[par], (qb - 1) * 128
                            else:
                                ksrc, kc0 = kT[1 - par], 384
                            sc = PB[1 + blk]
                            dep.group(PE, [
                                lambda: nc.tensor.matmul(
                                    sc.t[:, :], lhsT=ksrc.t[:, kc0:kc0 + 128],
                                    rhs=qz[g].t[:, :, qb * 128:(qb + 1) * 128], start=True, stop=False),
                                lambda: nc.tensor.matmul(
                                    sc.t[:, :], lhsT=ident.t[:, :], rhs=biasT.t[:, g * 2 + blk, :],
                                    start=False, stop=True),
                            ], reads=[ksrc, qz[g], ident, biasT], writes=[sc])
                            dep.op(ACT, lambda: nc.scalar.activation(
                                out=pt.t[:, g * 2 + blk, :], in_=sc.t[:, :], func=AF.Exp, scale=0.125),
                                reads=[sc], writes=[pt])
                        yield (1.7, 0)
                    pvo = [PB[3], PB[0]]
                    for g in range(2):
                        fns = []
                        rbufs = [pt]
                        for hh in range(4):
                            for bi, blk in enumerate(blks):
                                if blk == 1:
                                    vsrc, vb = vtok[par], qb
                                elif qb > 0:
                                    vsrc, vb = vtok[par], qb - 1
                                else:
                                    vsrc, vb = vtok[1 - par], 3
                                if vsrc not in rbufs:
                                    rbufs.append(vsrc)
                                fns.append(lambda hh=hh, blk=blk, vsrc=vsrc, vb=vb, bi=bi: nc.tensor.matmul(
                                    pvo[g].t[:, hh * 65:(hh + 1) * 65],
                                    lhsT=pt.t[:, g * 2 + blk, hh * 128:(hh + 1) * 128],
                                    rhs=vsrc.t[:, vb, g, :], start=(bi == 0), stop=(bi == len(blks) - 1)))
                        dep.group(PE, fns, reads=rbufs, writes=[pvo[g]])
                        pv3 = pvo[g].t[:, 0:260].rearrange("p (h d) -> p h d", d=65)
                        dep.op(DVE, lambda: nc.vector.tensor_tensor(
                            out=den.t[:, g * 4:(g + 1) * 4], in0=pv3[:, :, 64], in1=es_bc.t[:, g * 4:(g + 1) * 4],
                            op=ALU.add), reads=[pvo[g], es_bc], writes=[den])
                        yield (0.8, 0)
                    dep.op(DVE, lambda: nc.vector.reciprocal(out=rden.t[:, :], in_=den.t[:, :]),
                           reads=[den], writes=[rden])
                    for g in range(2):
                        pv3 = pvo[g].t[:, 0:260].rearrange("p (h d) -> p h d", d=65)
                        dep.op(DVE, lambda: nc.vector.tensor_tensor(
                            out=attn_n.t[:, g * 256:(g + 1) * 256].rearrange("p (h d) -> p h d", d=64),
                            in0=pv3[:, :, 0:64],
                            in1=rden.t[:, g * 4:(g + 1) * 4].unsqueeze(2).to_broadcast([128, 4, 64]),
                            op=ALU.mult), reads=[pvo[g], rden], writes=[attn_n])
                    inv = rms_inv_batch([(attn_n.t[:, :], attn_n)], 512, "Bq")
                    dep.op(DVE, lambda: nc.vector.scalar_tensor_tensor(
                        out=attn_s.t[:, :], in0=attn_n.t[:, :], scalar=inv.t[:, 0:1], in1=ga.t[:, :],
                        op0=ALU.mult, op1=ALU.mult), reads=[attn_n, inv, ga], writes=[attn_s])
                    yield (3.2, 0)
                    tpb = PB[0]
                    tpv = tpview(tpb)
                    dep.group(PE, [lambda e=e: nc.tensor.transpose(
                        out=tpv[:, e, :], in_=attn_s.t[:, e * 128:(e + 1) * 128], identity=ident.t[:, :])
                        for e in range(4)], reads=[attn_s, ident], writes=[tpb])
                    dep.op(ACT, lambda: nc.scalar.activation(
                        out=mixedT.t[:, 0:4, qb * 128:(qb + 1) * 128], in_=tpv[:, 0:4, :], func=AF.Copy),
                        reads=[tpb], writes=[mixedT])
                    yield (1.3, 0)

                while not try_load_x(t):
                    yield BLOCKED
                for dh in range(2):
                    n, Fc = stream.acquire(CH_F0 + dh)
                    Fv = Fc.t[:].rearrange("p (e d) -> p e d", d=512)
                    for tg in range(4):
                        bank = PB[1 + (tg % 2)]
                        dep.group(PE, [lambda e=e: nc.tensor.matmul(
                            bank.t[:, :], lhsT=mixedT.t[:, e, tg * 128:(tg + 1) * 128],
                            rhs=Fv[:, e, :], start=(e == 0), stop=(e == 7))
                            for e in range(8)], reads=[mixedT, Fc], writes=[bank])
                        dep.op(DVE, lambda: nc.vector.tensor_tensor(
                            out=xb.t[:, tg, dh * 512:(dh + 1) * 512], in0=bank.t[:, :],
                            in1=xb.t[:, tg, dh * 512:(dh + 1) * 512], op=ALU.add), reads=[bank, xb], writes=[xb])
                        yield (2.0, 0)
                    stream.release(n)
                flags["mixer_done"].add(t)

            def ffn(t):
                xb = X[t % 2]
                while t not in flags["mixer_done"]:
                    yield BLOCKED
                yield from norm_transpose(lambda g: (xb.t[:, g, :], xb), g2, zT, xsA, FA[3], "A", 4)
                for c in range(8):
                    n, G = stream.acquire(CH_G0 + c)
                    Gv = G.t[:].rearrange("p (k c) -> p k c", c=512)
                    for fi in range(4):
                        ft = c * 4 + fi
                        bank = FA[ft % 2]
                        proj_unit(bank, G, Gv, fi * 128, zT)
                        rl = relu_t[ft % 2]
                        dep.op(ACT, lambda: nc.scalar.activation(out=rl.t[:, :], in_=bank.t[:, :], func=AF.Relu),
                               reads=[bank], writes=[rl])
                        dep.op(DVE, lambda: nc.vector.tensor_tensor(
                            out=aT.t[:, ft, :], in0=bank.t[:, :], in1=rl.t[:, :], op=ALU.mult),
                            reads=[bank, rl], writes=[aT])
                        yield (2.0, 0)
                    stream.release(n)
                for dh in range(2):
                    for fg in range(4):
                        n, H = stream.acquire(CH_H0 + dh * 4 + fg)
                        Hv = H.t[:].rearrange("p (k c) -> p k c", c=512)
                        for fp in range(4):
                            fns = []
                            for fi in (2 * fp, 2 * fp + 1):
                                ft = fg * 8 + fi
                                for tg in range(4):
                                    fns.append(lambda fi=fi, ft=ft, tg=tg: nc.tensor.matmul(
                                        FA[tg].t[:, :], lhsT=aT.t[:, ft, tg * 128:(tg + 1) * 128],
                                        rhs=Hv[:, fi, :], start=(ft == 0), stop=(ft == 31)))
                            dep.group(PE, fns, reads=[aT, H], writes=[FA[0], FA[1], FA[2], FA[3]])
                            yield (2.0, 0)
                        stream.release(n)
                    for tg in range(4):
                        dep.op(DVE, lambda: nc.vector.tensor_tensor(
                            out=xb.t[:, tg, dh * 512:(dh + 1) * 512], in0=FA[tg].t[:, :],
                            in1=xb.t[:, tg, dh * 512:(dh + 1) * 512], op=ALU.add),
                            reads=[FA[tg], xb], writes=[xb])
                    yield (2.8, 0)
                inv = rms_inv_batch([(xb.t[:, g, :], xb) for g in range(4)], D, "Af")
                yield (5.4, 0)
                for g in range(4):
                    dep.op(DVE, lambda: nc.vector.scalar_tensor_tensor(
                        out=xb.t[:, g, :], in0=xb.t[:, g, :], scalar=inv.t[:, g:g + 1], in1=gf.t[:, :],
                        op0=ALU.mult, op1=ALU.mult), reads=[xb, inv, gf], writes=[xb])
                    yield (1.3, 0)
                store_x(t)

            TOT_A, TOT_B, LEAD = 161.6, 124.0, 1.1

            def chain(fn):
                for t in range(nt):
                    yield from fn(t)

            def schedule():
                gens = {"A": chain(ffn), "B": chain(mixer)}
                alive = {"A": nt > 0, "B": nt > 0}
                prog = {"A": 0.0, "B": 0.0}
                force_a = 0
                avoid = None
                blocked_streak = 0
                while alive["A"] or alive["B"]:
                    if not alive["B"]:
                        pick = "A"
                    elif not alive["A"]:
                        pick = "B"
                    elif avoid is not None:
                        pick = "B" if avoid == "A" else "A"
                    elif force_a > 0:
                        pick = "A"
                    else:
                        pick = "B" if (prog["B"] / TOT_B - prog["A"] / TOT_A) < LEAD else "A"
                    avoid = None
                    try:
                        r = next(gens[pick])
                    except StopIteration:
                        alive[pick] = False
                        continue
                    if r is BLOCKED:
                        blocked_streak += 1
                        assert blocked_streak < 4, "scheduler deadlock"
                        other = "B" if pick == "A" else "A"
                        assert alive[other], "blocked with no other thread"
                        avoid = pick
                        if pick == "A":
                            force_a = 0
                        continue
                    blocked_streak = 0
                    cost, gap = r
                    prog[pick] += cost
                    if pick == "A":
                        force_a = max(0, force_a - 1)
                    else:
                        force_a = gap

            if nt > 0:
                load_xin(0, 0)
                load_xin(0, 1)
            schedule()

        dep.dry = True
        rec = Stream(None)
        emit_all(rec)
        dep.dry = False
        emit_all(Stream(rec.order))

        for i in range(2):
            if Sx[i].count > 0:
                nc.gpsimd.wait_ge(Sx[i].sem, Sx[i].count)
    return nc


def _host_consts():
    ident = np.eye(128, dtype=np.float32)
    bias = np.zeros((128, 4, 4, 128), dtype=np.float32)
    s = np.arange(128)[:, None]
    q = np.arange(128)[None, :]
    for g in range(2):
        for hh in range(4):
            h = 4 * g + hh
            slope = 2.0 ** (-(h + 1))
            dist_prev = q - s + 128
            bias[:, g * 2 + 0, hh, :] = np.where(dist_prev < 128, -slope * 8.0 * dist_prev, -30000.0)
            dist_cur = q - s
            bias[:, g * 2 + 1, hh, :] = np.where(dist_cur >= 0, -slope * 8.0 * dist_cur, -30000.0)
    return ident, bias.reshape(128, 2048)


def make_in_maps(x, norm1_g, w_in, conv_w, sinks, attn_norm_g, conv_norm_g,
                 w_out, norm2_g, w_ff1, w_ff2, final_g):
    f = lambda a: np.ascontiguousarray(np.asarray(a, dtype=np.float32))
    ident, bias = _host_consts()
    bc = lambda v: f(np.broadcast_to(np.asarray(v, dtype=np.float32).reshape(1, -1), (128, np.asarray(v).size)))
    common = {
        "w_in": f(w_in[0]), "w_out": f(w_out[0]), "w_ff1": f(w_ff1[0]), "w_ff2": f(w_ff2[0]),
        "g1": bc(norm1_g[0]), "g2": bc(norm2_g[0]), "gf": bc(final_g), "ga": bc(attn_norm_g[0]),
        "gc": f(np.asarray(conv_norm_g[0]).reshape(4, 128).T),
        "cw": f(np.asarray(conv_w[0]).reshape(3, 4, 128).transpose(2, 1, 0).reshape(128, 12)),
        "sk": bc(sinks[0]), "ident": ident, "bias": bias,
    }
    x = np.asarray(x, dtype=np.float32)
    maps = []
    for i in range(NCORES):
        m = dict(common)
        m["x"] = np.ascontiguousarray(x[2 * i:2 * i + 2].reshape(2 * SEQ, D))
        maps.append(m)
    return maps


def kernel(x, norm1_g, w_in, conv_w, sinks, attn_norm_g, conv_norm_g,
           w_out, norm2_g, w_ff1, w_ff2, final_g):
    maps = make_in_maps(x, norm1_g, w_in, conv_w, sinks, attn_norm_g, conv_norm_g,
                        w_out, norm2_g, w_ff1, w_ff2, final_g)
    nc = build(NT)
    res = run_bass_kernel_spmd(nc, maps, core_ids=list(range(NCORES)))
    outs = [np.asarray(r["out"], dtype=np.float32).reshape(2, SEQ, D) for r in res.results]
    return np.concatenate(outs, axis=0)
```

```python
import numpy as np
from contextlib import ExitStack
import concourse.bass as bass
import concourse.mybir as mybir
from concourse.bass_utils import run_bass_kernel_spmd

F32 = mybir.dt.float32
BF16 = mybir.dt.bfloat16
AF = mybir.ActivationFunctionType
ALU = mybir.AluOpType

NCORES = 8
D = 1024
SEQ = 4096
T = 512
NT = 16
TPS = 8
EPS = 1e-5
NS = 5
CH_Q, CH_KV, CH_V0, CH_F0, CH_G0, CH_H0 = 0, 1, 2, 6, 8, 16
NCHUNK = 24
CH_SIZE = [4096] * NCHUNK
CH_SIZE[CH_KV] = 2048
for _i in range(4):
    CH_SIZE[CH_V0 + _i] = 3072


class Tok:
    __slots__ = ("sem", "val", "eng")

    def __init__(self, sem, val, eng):
        self.sem, self.val, self.eng = sem, val, eng


class Buf:
    def __init__(self, name, t=None):
        self.name, self.t = name, t
        self.w = None
        self.r = {}


class Eng:
    def __init__(self, nc, h, name, es, is_pe=False):
        self.h = h
        self.name = name
        self.sem = es.enter_context(nc.semaphore("e_" + name))
        self.count = 0
        self.waited = {}
        self.is_pe = is_pe


class DSem:
    def __init__(self, nc, name, es):
        self.sem = es.enter_context(nc.semaphore("d_" + name))
        self.count = 0


class Dep:
    def wait(self, eng, reads, writes):
        need = {}

        def add(tok, raw):
            if tok is None:
                return
            if tok.eng is eng:
                if eng.is_pe or not raw:
                    return
            k = id(tok.sem)
            if k not in need or need[k].val < tok.val:
                need[k] = tok

        for b in reads:
            add(b.w, True)
        for b in writes:
            add(b.w, False)
            for t in b.r.values():
                add(t, False)
        for k, tok in need.items():
            if eng.waited.get(k, 0) < tok.val:
                eng.h.wait_ge(tok.sem, tok.val)
                eng.waited[k] = tok.val

    def _update(self, tok, reads, writes):
        for b in writes:
            b.w = tok
            b.r = {}
        for b in reads:
            b.r[id(tok.sem)] = tok

    dry = False

    def op(self, eng, fn, reads=(), writes=()):
        if self.dry:
            return
        self.wait(eng, reads, writes)
        inst = fn()
        eng.count += 1
        inst.then_inc(eng.sem, 1)
        self._update(Tok(eng.sem, eng.count, eng), reads, writes)

    def group(self, eng, fns, reads=(), writes=()):
        if self.dry:
            return
        self.wait(eng, reads, writes)
        inst = None
        for fn in fns:
            inst = fn()
        eng.count += 1
        inst.then_inc(eng.sem, 1)
        self._update(Tok(eng.sem, eng.count, eng), reads, writes)

    def dma(self, qeng, fns, dsem, reads=(), writes=()):
        if self.dry:
            return
        self.wait(qeng, reads, writes)
        for fn in fns:
            fn().then_inc(dsem.sem, 16)
            dsem.count += 16
        self._update(Tok(dsem.sem, dsem.count, None), reads, writes)


def build(nt=NT, do_prologue=True):
    nc = bass.Bass("TRN2", target_bir_lowering=False)

    def din(name, shape):
        return nc.dram_tensor(name, shape, F32, kind="ExternalInput").ap()

    x_d = din("x", [2 * SEQ, D])
    w_in_d = din("w_in", [D, 2304])
    w_out_d = din("w_out", [D, D])
    w_ff1_d = din("w_ff1", [D, 4096])
    w_ff2_d = din("w_ff2", [4096, D])
    g1_d = din("g1", [128, D])
    g2_d = din("g2", [128, D])
    gf_d = din("gf", [128, D])
    ga_d = din("ga", [128, 512])
    gc_d = din("gc", [128, 4])
    cw_d = din("cw", [128, 12])
    sk_d = din("sk", [128, 8])
    id_d = din("ident", [128, 128])
    bias_d = din("bias", [128, 2048])
    out_d = nc.dram_tensor("out", [2 * SEQ, D], F32, kind="ExternalOutput").ap()
    wscr = nc.dram_tensor("wscr", [NCHUNK, 128, 4096], BF16).ap()

    dep = Dep()
    with ExitStack() as es:
        def sb(name, shape, dt):
            return Buf(name, es.enter_context(nc.sbuf_tensor("s_" + name, shape, dt)))

        def ps(name, shape, dt):
            return Buf(name, es.enter_context(nc.psum_tensor("p_" + name, shape, dt)))

        PE = Eng(nc, nc.tensor, "pe", es, is_pe=True)
        ACT = Eng(nc, nc.scalar, "act", es)
        DVE = Eng(nc, nc.vector, "dve", es)
        POOL = Eng(nc, nc.gpsimd, "pool", es)
        SP = Eng(nc, nc.sync, "sp", es)

        X = [sb(f"X{i}", [128, 4, D], F32) for i in range(2)]
        junk = es.enter_context(nc.sbuf_tensor("junk", [128, D], BF16))
        xsB = sb("xsB", [128, D], BF16)
        xsA = sb("xsA", [128, D], BF16)
        xT = sb("xT", [128, 8, T], BF16)
        zT = sb("zT", [128, 8, T], BF16)
        qz = [sb(f"qz{i}", [128, 4, T], BF16) for i in range(2)]
        kT = [sb(f"kT{i}", [128, T], BF16) for i in range(2)]
        vtok = [sb(f"vtok{i}", [128, 4, 2, 65], BF16) for i in range(2)]
        u_sb = sb("u_sb", [128, T], F32)
        cu = sb("cu", [128, 4, T + 2], F32)
        conv_f = sb("conv_f", [128, 4, T], F32)
        sq = sb("sq", [128, T], BF16)
        invc = sb("invc", [128, T], F32)
        mixedT = sb("mixedT", [128, 8, T], BF16)
        PT = sb("PT", [128, 4, 512], BF16)
        xin = [sb(f"xin{i}", [128, D], F32) for i in range(2)]
        attn_n = sb("attn_n", [128, 512], F32)
        attn_s = sb("attn_s", [128, 512], BF16)
        aT = sb("aT", [128, 32, T], BF16)
        relu_t = [sb(f"relu{i}", [128, T], F32) for i in range(2)]
        R = [sb(f"ring{i}", [128, 4096], BF16) for i in range(NS)]
        ident = sb("ident", [128, 128], BF16)
        ones_bf = sb("ones_bf", [128, 128], BF16)
        biasT = sb("biasT", [128, 4, 512], BF16)
        g1 = sb("g1", [128, D], F32)
        g2 = sb("g2", [128, D], F32)
        gf = sb("gf", [128, D], F32)
        ga = sb("ga", [128, 512], F32)
        gc = sb("gc", [128, 4], F32)
        cw = sb("cw", [128, 12], F32)
        es_bc = sb("es_bc", [128, 8], F32)
        epsc = sb("epsc", [128, 1], F32)
        stats = {}
        for key in ("A0", "B0", "B1", "Bq", "Af"):
            stats[key] = [sb(f"st_{key}{i}", [128, 4], F32) for i in range(3)]
        den = sb("den", [128, 8], F32)
        rden = sb("rden", [128, 8], F32)

        FA = [ps(f"FA{i}", [128, 512], F32) for i in range(4)]
        PB = [ps(f"PB{i}", [128, 512], F32) for i in range(4)]

        def tpview(bank):
            return bank.t[:, :].bitcast(BF16).rearrange("p (k j) -> p k j", j=128)

        Lx = [DSem(nc, f"lx{i}", es) for i in range(2)]
        Lin = [DSem(nc, f"lin{i}", es) for i in range(2)]
        Sx = [DSem(nc, f"sx{i}", es) for i in range(2)]
        RS = [DSem(nc, f"rs{i}", es) for i in range(NS)]
        RST = [DSem(nc, f"rst{i}", es) for i in range(NS)]
        CS = DSem(nc, "cs", es)
        WS = [Buf(f"ws{c}") for c in range(NCHUNK)]

        cstage = X[1]
        cst = cstage.t[:].rearrange("p a b -> p (a b)")
        dep.dma(SP, [
            lambda: nc.sync.dma_start(out=cst[:, 0:128], in_=id_d[:, :]),
            lambda: nc.sync.dma_start(out=cst[:, 128:128 + 2048], in_=bias_d[:, :]),
        ], Lx[1], writes=[cstage])
        dep.dma(SP, [
            lambda: nc.sync.dma_start(out=g1.t[:, :], in_=g1_d[:, :]),
            lambda: nc.sync.dma_start(out=g2.t[:, :], in_=g2_d[:, :]),
            lambda: nc.sync.dma_start(out=gf.t[:, :], in_=gf_d[:, :]),
            lambda: nc.sync.dma_start(out=ga.t[:, :], in_=ga_d[:, :]),
            lambda: nc.sync.dma_start(out=gc.t[:, :], in_=gc_d[:, :]),
            lambda: nc.sync.dma_start(out=cw.t[:, :], in_=cw_d[:, :]),
            lambda: nc.sync.dma_start(out=es_bc.t[:, :], in_=sk_d[:, :]),
        ], CS, writes=[g1, g2, gf, ga, gc, cw, es_bc])
        dep.op(DVE, lambda: nc.vector.tensor_copy(out=ident.t[:, :], in_=cst[:, 0:128]),
               reads=[cstage], writes=[ident])
        dep.op(DVE, lambda: nc.vector.tensor_copy(
            out=biasT.t[:].rearrange("p a b -> p (a b)"), in_=cst[:, 128:128 + 2048]),
            reads=[cstage], writes=[biasT])
        dep.op(DVE, lambda: nc.vector.memset(ones_bf.t[:, :], 1.0), writes=[ones_bf])
        dep.op(DVE, lambda: nc.vector.memset(epsc.t[:, :], EPS), writes=[epsc])
        for i in range(2):
            dep.op(DVE, lambda i=i: nc.vector.memset(
                vtok[i].t[:].rearrange("p a b c -> p (a b c)"), 1.0), writes=[vtok[i]])
            dep.op(DVE, lambda i=i: nc.vector.memset(
                qz[i].t[:].rearrange("p a b -> p (a b)"), 0.0), writes=[qz[i]])
        dep.op(ACT, lambda: nc.scalar.activation(out=es_bc.t[:, :], in_=es_bc.t[:, :], func=AF.Exp),
               reads=[es_bc], writes=[es_bc])

        def src_views(c):
            res = []
            wv_in = w_in_d.rearrange("(k p) c -> p k c", p=128)
            if c == CH_Q:
                for j in range(4):
                    for half in range(2):
                        col = (half * 4 + j) * 64
                        res.append((lambda s, j=j, half=half: s.rearrange("p (k c) -> p k c", c=512)[
                            :, :, j * 128 + half * 64: j * 128 + half * 64 + 64],
                            wv_in[:, :, col:col + 64]))
            elif c == CH_KV:
                res.append((lambda s: s[:, 0:2048].rearrange("p (k c) -> p k c", c=256),
                            wv_in[:, :, 512:768]))
            elif CH_V0 <= c < CH_V0 + 4:
                ct = c - CH_V0
                for ui, c0 in enumerate((1792, 768, 1280)):
                    res.append((lambda s, ui=ui: s[:, 0:3072].rearrange("p (k c) -> p k c", c=384)[
                        :, :, ui * 128:(ui + 1) * 128],
                        wv_in[:, :, c0 + ct * 128:c0 + (ct + 1) * 128]))
            elif c in (CH_F0, CH_F0 + 1):
                dh = c - CH_F0
                wv = w_out_d.rearrange("(e p) d -> p e d", p=128)
                res.append((lambda s: s.rearrange("p (e d) -> p e d", d=512), wv[:, :, dh * 512:(dh + 1) * 512]))
            elif CH_G0 <= c < CH_H0:
                j = c - CH_G0
                wv = w_ff1_d.rearrange("(k p) f -> p k f", p=128)
                res.append((lambda s: s.rearrange("p (k c) -> p k c", c=512), wv[:, :, j * 512:(j + 1) * 512]))
            else:
                j = c - CH_H0
                dh, fg = j // 4, j % 4
                wv = w_ff2_d.rearrange("(f p) d -> p f d", p=128)
                res.append((lambda s: s.rearrange("p (k c) -> p k c", c=512),
                            wv[:, fg * 8:(fg + 1) * 8, dh * 512:(dh + 1) * 512]))
            return res

        PCS = [DSem(nc, f"pc{c}", es) for c in range(NCHUNK)]

        pro_state = {"next": 0}

        def ensure_prologue(cmax):
            if not do_prologue:
                return
            while pro_state["next"] <= cmax:
                c = pro_state["next"]
                pro_state["next"] += 1
                n = CH_SIZE[c]
                fns = []
                for (dv, src) in src_views(c):
                    fns.append(lambda dv=dv, src=src, c=c, n=n: nc.gpsimd.dma_start(
                        out=dv(wscr[c, :, 0:n]) if n == 4096 else dv(wscr[c]), in_=src))
                dep.dma(POOL, fns, PCS[c], writes=[WS[c]])

        class Stream:
            def __init__(self, order=None):
                self.record = order is None
                self.order = [] if order is None else order
                self.acq = 0
                self.issued = 0
                self.free = list(range(NS))
                self.slot_of = {}
                if not self.record:
                    self.pump(limit=2)

            def pump(self, limit=None):
                while self.issued < len(self.order) and self.free and (limit is None or self.issued < limit):
                    n = self.issued
                    c = self.order[n]
                    slot = self.free.pop(0)
                    self.slot_of[n] = slot
                    sz = CH_SIZE[c]
                    ensure_prologue(c)
                    dep.dma(SP, [lambda: nc.sync.dma_start(out=R[slot].t[:, 0:sz], in_=wscr[c, :, 0:sz])],
                            RS[slot], reads=[WS[c]], writes=[R[slot]])
                    self.issued += 1

            def acquire(self, c):
                n = self.acq
                self.acq += 1
                if self.record:
                    self.order.append(c)
                    return n, R[0]
                self.pump()
                assert self.order[n] == c, (n, c, self.order[n])
                assert n < self.issued, ("ring too small / order deadlock", n, self.issued)
                return n, R[self.slot_of[n]]

            def release(self, n):
                if self.record:
                    return
                self.free.append(self.slot_of[n])
                self.pump()

        def emit_all(stream):
            BLOCKED = ("blocked",)
            flags = {"mixer_done": set(), "store_done": set(), "x_loaded": set()}

            def load_xin(t, g):
                seq, tb = t // TPS, t % TPS
                r0 = seq * SEQ + tb * T + g * 128
                b = xin[g % 2]
                dep.dma(POOL, [lambda: nc.gpsimd.dma_start(out=b.t[:, :], in_=x_d[r0:r0 + 128, :])],
                        Lin[g % 2], writes=[b])

            def load_x(t):
                seq, tb = t // TPS, t % TPS
                r0 = seq * SEQ + tb * T
                xb = X[t % 2]
                dep.dma(POOL, [lambda: nc.gpsimd.dma_start(
                    out=xb.t[:, :, :], in_=x_d[r0:r0 + T, :].rearrange("(g p) d -> p g d", p=128))],
                    Lx[t % 2], writes=[xb])
                flags["x_loaded"].add(t)

            def try_load_x(t):
                if t in flags["x_loaded"]:
                    return True
                if t >= 2 and (t - 2) not in flags["store_done"]:
                    return False
                load_x(t)
                return True

            def store_x(t):
                seq, tb = t // TPS, t % TPS
                r0 = seq * SEQ + tb * T
                xb = X[t % 2]
                dep.dma(POOL, [lambda: nc.gpsimd.dma_start(
                    out=out_d[r0:r0 + T, :].rearrange("(g p) d -> p g d", p=128), in_=xb.t[:, :, :])],
                    Sx[t % 2], reads=[xb])
                flags["store_done"].add(t)

            def rms_inv_batch(srcs, n_feat, key):
                ss, ln_, inv = stats[key]
                nb_ = len(srcs)
                for i, (sap, sbuf) in enumerate(srcs):
                    dep.op(ACT, lambda: nc.scalar.activation(
                        out=junk[:, 0:n_feat], in_=sap, func=AF.Square, accum_out=ss.t[:, i:i + 1]),
                        reads=[sbuf], writes=[ss])
                dep.op(ACT, lambda: nc.scalar.activation(
                    out=ln_.t[:, 0:nb_], in_=ss.t[:, 0:nb_], func=AF.Ln, scale=1.0 / n_feat, bias=epsc.t[:, 0:1]),
                    reads=[ss, epsc], writes=[ln_])
                dep.op(ACT, lambda: nc.scalar.activation(
                    out=inv.t[:, 0:nb_], in_=ln_.t[:, 0:nb_], func=AF.Exp, scale=-0.5),
                    reads=[ln_], writes=[inv])
                return inv

            def norm_transpose(src, gtab, dstT, xsb, bank, th, batch, after_stt=None):
                tpv = tpview(bank)
                for bi, g0 in enumerate(range(0, 4, batch)):
                    gs = list(range(g0, g0 + batch))
                    inv = rms_inv_batch([src(g) for g in gs], D, th + str(bi))
                    yield (1.2 * batch + 0.6, 0)
                    for i, g in enumerate(gs):
                        sap, sbuf = src(g)
                        dep.op(DVE, lambda: nc.vector.scalar_tensor_tensor(
                            out=xsb.t[:, :], in0=sap, scalar=inv.t[:, i:i + 1], in1=gtab.t[:, :],
                            op0=ALU.mult, op1=ALU.mult), reads=[sbuf, inv, gtab], writes=[xsb])
                        if after_stt is not None:
                            after_stt(g)
                        yield (1.3, 0)
                        dep.group(PE, [lambda k=k: nc.tensor.transpose(
                            out=tpv[:, k, :], in_=xsb.t[:, k * 128:(k + 1) * 128], identity=ident.t[:, :])
                            for k in range(8)], reads=[xsb, ident], writes=[bank])
                        dep.op(ACT, lambda: nc.scalar.activation(
                            out=dstT.t[:, :, g * 128:(g + 1) * 128], in_=tpv[:, :, :], func=AF.Copy),
                            reads=[bank], writes=[dstT])
                        yield (1.7, 0)

            def proj_unit(bank, wslot, wview, col0, srcT):
                dep.group(PE, [lambda k=k: nc.tensor.matmul(
                    bank.t[:, :], lhsT=wview[:, k, col0:col0 + 128], rhs=srcT.t[:, k, :],
                    start=(k == 0), stop=(k == 7)) for k in range(8)],
                    reads=[wslot, srcT], writes=[bank])

            def mixer(t):
                seq, tb = t // TPS, t % TPS
                xb = X[t % 2]
                par = t % 2
                try_load_x(t)

                def after_stt(g):
                    if g + 2 < 4:
                        load_xin(t, g + 2)
                    elif t + 1 < nt:
                        load_xin(t + 1, g - 2)

                yield from norm_transpose(lambda g: (xin[g % 2].t[:, :], xin[g % 2]), g1, xT, xsB, PB[0], "B", 2,
                                          after_stt)
                n, Qc = stream.acquire(CH_Q)
                Qv = Qc.t[:].rearrange("p (k c) -> p k c", c=512)
                for j in range(4):
                    bank = PB[1 + (j % 2)]
                    proj_unit(bank, Qc, Qv, j * 128, xT)
                    dep.op(ACT, lambda: nc.scalar.activation(
                        out=qz[0].t[0:64, j, :], in_=bank.t[0:64, :], func=AF.Copy),
                        reads=[bank], writes=[qz[0]])
                    dep.op(DVE, lambda: nc.vector.tensor_copy(
                        out=qz[1].t[64:128, j, :], in_=bank.t[64:128, :]), reads=[bank], writes=[qz[1]])
                    yield (2.0, 0)
                stream.release(n)
                n, KVc = stream.acquire(CH_KV)
                KVv = KVc.t[:, 0:2048].rearrange("p (k c) -> p k c", c=256)
                bank = PB[1]
                proj_unit(bank, KVc, KVv, 0, xT)
                dep.op(DVE, lambda: nc.vector.tensor_copy(out=kT[par].t[:, :], in_=bank.t[:, :]),
                       reads=[bank], writes=[kT[par]])
                yield (2.0, 0)
                bank = PB[2]
                fns = []
                for tg in range(4):
                    for k in range(8):
                        fns.append(lambda tg=tg, k=k: nc.tensor.matmul(
                            bank.t[:, tg * 128:(tg + 1) * 128], lhsT=xT.t[:, k, tg * 128:(tg + 1) * 128],
                            rhs=KVv[:, k, 128:256], start=(k == 0), stop=(k == 7)))
                dep.group(PE, fns, reads=[KVc, xT], writes=[bank])
                dep.op(DVE, lambda: nc.vector.tensor_copy(
                    out=vtok[par].t[:, :, :, 0:64],
                    in_=bank.t[:, :].rearrange("p (a b c) -> p a b c", a=4, b=2)),
                    reads=[bank], writes=[vtok[par]])
                stream.release(n)
                yield (3.0, 0)
                if tb == 0:
                    dep.op(DVE, lambda: nc.vector.memset(cu.t[:, :, 0:2], 0.0), writes=[cu])
                ssc = PB[3]
                for ct in range(4):
                    n, Vc = stream.acquire(CH_V0 + ct)
                    Vv = Vc.t[:, 0:3072].rearrange("p (k c) -> p k c", c=384)
                    bank = PB[1]
                    proj_unit(bank, Vc, Vv, 0, xT)
                    dep.op(ACT, lambda: nc.scalar.activation(out=u_sb.t[:, :], in_=bank.t[:, :], func=AF.Copy),
                           reads=[bank], writes=[u_sb])
                    yield (2.0, 0)
                    bank = PB[2]
                    proj_unit(bank, Vc, Vv, 128, xT)
                    dep.op(DVE, lambda: nc.vector.tensor_tensor(
                        out=cu.t[:, ct, 2:T + 2], in0=bank.t[:, :], in1=u_sb.t[:, :], op=ALU.mult),
                        reads=[bank, u_sb], writes=[cu])
                    y = conv_f.t[:, ct, :]
                    dep.op(DVE, lambda: nc.vector.tensor_scalar(
                        out=y, in0=cu.t[:, ct, 2:T + 2], scalar1=cw.t[:, ct * 3 + 2:ct * 3 + 3], scalar2=None,
                        op0=ALU.mult), reads=[cu, cw], writes=[conv_f])
                    dep.op(DVE, lambda: nc.vector.scalar_tensor_tensor(
                        out=y, in0=cu.t[:, ct, 1:T + 1], scalar=cw.t[:, ct * 3 + 1:ct * 3 + 2], in1=y,
                        op0=ALU.mult, op1=ALU.add), reads=[cu, cw, conv_f], writes=[conv_f])
                    dep.op(DVE, lambda: nc.vector.scalar_tensor_tensor(
                        out=y, in0=cu.t[:, ct, 0:T], scalar=cw.t[:, ct * 3:ct * 3 + 1], in1=y,
                        op0=ALU.mult, op1=ALU.add), reads=[cu, cw, conv_f], writes=[conv_f])
                    yield (4.0, 0)
                    bank = PB[1]
                    proj_unit(bank, Vc, Vv, 256, xT)
                    stream.release(n)
                    dep.op(DVE, lambda: nc.vector.tensor_tensor(
                        out=y, in0=bank.t[:, :], in1=y, op=ALU.mult), reads=[bank, conv_f], writes=[conv_f])
                    dep.op(ACT, lambda: nc.scalar.activation(out=sq.t[:, :], in_=y, func=AF.Square),
                           reads=[conv_f], writes=[sq])
                    yield (3.3, 0)
                    dep.group(PE, [lambda: nc.tensor.matmul(
                        ssc.t[:, :], lhsT=ones_bf.t[:, :], rhs=sq.t[:, :], start=(ct == 0), stop=(ct == 3))],
                        reads=[sq, ones_bf], writes=[ssc])
                    yield (0.3, 0)
                if tb != TPS - 1:
                    dep.op(DVE, lambda: nc.vector.tensor_copy(out=cu.t[:, :, 0:2], in_=cu.t[:, :, T:T + 2]),
                           reads=[cu], writes=[cu])
                dep.op(ACT, lambda: nc.scalar.activation(
                    out=invc.t[:, :], in_=ssc.t[:, :], func=AF.Ln, scale=1.0 / 512, bias=epsc.t[:, 0:1]),
                    reads=[ssc, epsc], writes=[invc])
                dep.op(ACT, lambda: nc.scalar.activation(
                    out=invc.t[:, :], in_=invc.t[:, :], func=AF.Exp, scale=-0.5), reads=[invc], writes=[invc])
                for ct in range(4):
                    dep.op(DVE, lambda: nc.vector.scalar_tensor_tensor(
                        out=mixedT.t[:, 4 + ct, :], in0=conv_f.t[:, ct, :], scalar=gc.t[:, ct:ct + 1],
                        in1=invc.t[:, :], op0=ALU.mult, op1=ALU.mult),
                        reads=[conv_f, gc, invc], writes=[mixedT])
                try_load_x(t)
                yield (2.0, 0)

                for qb in range(4):
                    nb = tb * 4 + qb
                    blks = [1] if nb == 0 else [0, 1]
                    pt = PT
                    for g in range(2):
                        for blk in blks:
                            if blk == 1:
                                ksrc, kc0 = kT[par], qb * 128
                            elif qb > 0:
                                ksrc, kc0 = kT[par], (qb - 1) * 128
                            else:
                                ksrc, kc0 = kT[1 - par], 384
                            sc = PB[1 + blk]
                            dep.group(PE, [
                                lambda: nc.tensor.matmul(
                                    sc.t[:, :], lhsT=ksrc.t[:, kc0:kc0 + 128],
                                    rhs=qz[g].t[:, :, qb * 128:(qb + 1) * 128], start=True, stop=False),
                                lambda: nc.tensor.matmul(
                                    sc.t[:, :], lhsT=ident.t[:, :], rhs=biasT.t[:, g * 2 + blk, :],
                                    start=False, stop=True),
                            ], reads=[ksrc, qz[g], ident, biasT], writes=[sc])
                            dep.op(ACT, lambda: nc.scalar.activation(
                                out=pt.t[:, g * 2 + blk, :], in_=sc.t[:, :], func=AF.Exp, scale=0.125),
                                reads=[sc], writes=[pt])
                        yield (1.7, 0)
                    pvo = [PB[3], PB[0]]
                    for g in range(2):
                        fns = []
                        rbufs = [pt]
                        for hh in range(4):
                            for bi, blk in enumerate(blks):
                                if blk == 1:
                                    vsrc, vb = vtok[par], qb
                                elif qb > 0:
                                    vsrc, vb = vtok[par], qb - 1
                                else:
                                    vsrc, vb = vtok[1 - par], 3
                                if vsrc not in rbufs:
                                    rbufs.append(vsrc)
                                fns.append(lambda hh=hh, blk=blk, vsrc=vsrc, vb=vb, bi=bi: nc.tensor.matmul(
                                    pvo[g].t[:, hh * 65:(hh + 1) * 65],
                                    lhsT=pt.t[:, g * 2 + blk, hh * 128:(hh + 1) * 128],
                                    rhs=vsrc.t[:, vb, g, :], start=(bi == 0), stop=(bi == len(blks) - 1)))
                        dep.group(PE, fns, reads=rbufs, writes=[pvo[g]])
                        pv3 = pvo[g].t[:, 0:260].rearrange("p (h d) -> p h d", d=65)
                        dep.op(DVE, lambda: nc.vector.tensor_tensor(
                            out=den.t[:, g * 4:(g + 1) * 4], in0=pv3[:, :, 64], in1=es_bc.t[:, g * 4:(g + 1) * 4],
                            op=ALU.add), reads=[pvo[g], es_bc], writes=[den])
                        yield (0.8, 0)
                    dep.op(DVE, lambda: nc.vector.reciprocal(out=rden.t[:, :], in_=den.t[:, :]),
                           reads=[den], writes=[rden])
                    for g in range(2):
                        pv3 = pvo[g].t[:, 0:260].rearrange("p (h d) -> p h d", d=65)
                        dep.op(DVE, lambda: nc.vector.tensor_tensor(
                            out=attn_n.t[:, g * 256:(g + 1) * 256].rearrange("p (h d) -> p h d", d=64),
                            in0=pv3[:, :, 0:64],
                            in1=rden.t[:, g * 4:(g + 1) * 4].unsqueeze(2).to_broadcast([128, 4, 64]),
                            op=ALU.mult), reads=[pvo[g], rden], writes=[attn_n])
                    inv = rms_inv_batch([(attn_n.t[:, :], attn_n)], 512, "Bq")
                    dep.op(DVE, lambda: nc.vector.scalar_tensor_tensor(
                        out=attn_s.t[:, :], in0=attn_n.t[:, :], scalar=inv.t[:, 0:1], in1=ga.t[:, :],
                        op0=ALU.mult, op1=ALU.mult), reads=[attn_n, inv, ga], writes=[attn_s])
                    yield (3.2, 0)
                    tpb = PB[0]
                    tpv = tpview(tpb)
                    dep.group(PE, [lambda e=e: nc.tensor.transpose(
                        out=tpv[:, e, :], in_=attn_s.t[:, e * 128:(e + 1) * 128], identity=ident.t[:, :])
                        for e in range(4)], reads=[attn_s, ident], writes=[tpb])
                    dep.op(ACT, lambda: nc.scalar.activation(
                        out=mixedT.t[:, 0:4, qb * 128:(qb + 1) * 128], in_=tpv[:, 0:4, :], func=AF.Copy),
                        reads=[tpb], writes=[mixedT])
                    yield (1.3, 0)

                while not try_load_x(t):
                    yield BLOCKED
                for dh in range(2):
                    n, Fc = stream.acquire(CH_F0 + dh)
                    Fv = Fc.t[:].rearrange("p (e d) -> p e d", d=512)
                    for tg in range(4):
                        bank = PB[1 + (tg % 2)]
                        dep.group(PE, [lambda e=e: nc.tensor.matmul(
                            bank.t[:, :], lhsT=mixedT.t[:, e, tg * 128:(tg + 1) * 128],
                            rhs=Fv[:, e, :], start=(e == 0), stop=(e == 7))
                            for e in range(8)], reads=[mixedT, Fc], writes=[bank])
                        dep.op(DVE, lambda: nc.vector.tensor_tensor(
                            out=xb.t[:, tg, dh * 512:(dh + 1) * 512], in0=bank.t[:, :],
                            in1=xb.t[:, tg, dh * 512:(dh + 1) * 512], op=ALU.add), reads=[bank, xb], writes=[xb])
                        yield (2.0, 0)
                    stream.release(n)
                flags["mixer_done"].add(t)

            def ffn(t):
                xb = X[t % 2]
                while t not in flags["mixer_done"]:
                    yield BLOCKED
                yield from norm_transpose(lambda g: (xb.t[:, g, :], xb), g2, zT, xsA, FA[3], "A", 4)
                for c in range(8):
                    n, G = stream.acquire(CH_G0 + c)
                    Gv = G.t[:].rearrange("p (k c) -> p k c", c=512)
                    for fi in range(4):
                        ft = c * 4 + fi
                        bank = FA[ft % 2]
                        proj_unit(bank, G, Gv, fi * 128, zT)
                        rl = relu_t[ft % 2]
                        dep.op(ACT, lambda: nc.scalar.activation(out=rl.t[:, :], in_=bank.t[:, :], func=AF.Relu),
                               reads=[bank], writes=[rl])
                        dep.op(DVE, lambda: nc.vector.tensor_tensor(
                            out=aT.t[:, ft, :], in0=bank.t[:, :], in1=rl.t[:, :], op=ALU.mult),
                            reads=[bank, rl], writes=[aT])
                        yield (2.0, 0)
                    stream.release(n)
                for dh in range(2):
                    for fg in range(4):
                        n, H = stream.acquire(CH_H0 + dh * 4 + fg)
                        Hv = H.t[:].rearrange("p (k c) -> p k c", c=512)
                        for fp in range(4):
                            fns = []
                            for fi in (2 * fp, 2 * fp + 1):
                                ft = fg * 8 + fi
                                for tg in range(4):
                                    fns.append(lambda fi=fi, ft=ft, tg=tg: nc.tensor.matmul(
                                        FA[tg].t[:, :], lhsT=aT.t[:, ft, tg * 128:(tg + 1) * 128],
                                        rhs=Hv[:, fi, :], start=(ft == 0), stop=(ft == 31)))
                            dep.group(PE, fns, reads=[aT, H], writes=[FA[0], FA[1], FA[2], FA[3]])
                            yield (2.0, 0)
                        stream.release(n)
                    for tg in range(4):
                        dep.op(DVE, lambda: nc.vector.tensor_tensor(
                            out=xb.t[:, tg, dh * 512:(dh + 1) * 512], in0=FA[tg].t[:, :],
                            in1=xb.t[:, tg, dh * 512:(dh + 1) * 512], op=ALU.add),
                            reads=[FA[tg], xb], writes=[xb])
                    yield (2.8, 0)
                inv = rms_inv_batch([(xb.t[:, g, :], xb) for g in range(4)], D, "Af")
                yield (5.4, 0)
                for g in range(4):
                    dep.op(DVE, lambda: nc.vector.scalar_tensor_tensor(
                        out=xb.t[:, g, :], in0=xb.t[:, g, :], scalar=inv.t[:, g:g + 1], in1=gf.t[:, :],
                        op0=ALU.mult, op1=ALU.mult), reads=[xb, inv, gf], writes=[xb])
                    yield (1.3, 0)
                store_x(t)

            TOT_A, TOT_B, LEAD = 161.6, 124.0, 1.3

            def chain(fn):
                for t in range(nt):
                    yield from fn(t)

            def schedule():
                gens = {"A": chain(ffn), "B": chain(mixer)}
                alive = {"A": nt > 0, "B": nt > 0}
                prog = {"A": 0.0, "B": 0.0}
                force_a = 0
                avoid = None
                blocked_streak = 0
                while alive["A"] or alive["B"]:
                    if not alive["B"]:
                        pick = "A"
                    elif not alive["A"]:
                        pick = "B"
                    elif avoid is not None:
                        pick = "B" if avoid == "A" else "A"
                    elif force_a > 0:
                        pick = "A"
                    else:
                        pick = "B" if (prog["B"] / TOT_B - prog["A"] / TOT_A) < LEAD else "A"
                    avoid = None
                    try:
                        r = next(gens[pick])
                    except StopIteration:
                        alive[pick] = False
                        continue
                    if r is BLOCKED:
                        blocked_streak += 1
                        assert blocked_streak < 4, "scheduler deadlock"
                        other = "B" if pick == "A" else "A"
                        assert alive[other], "blocked with no other thread"
                        avoid = pick
                        if pick == "A":
                            force_a = 0
                        continue
                    blocked_streak = 0
                    cost, gap = r
                    prog[pick] += cost
                    if pick == "A":
                        force_a = max(0, force_a - 1)
                    else:
                        force_a = gap

            if nt > 0:
                load_xin(0, 0)
                load_xin(0, 1)
            schedule()

        dep.dry = True
        rec = Stream(None)
        emit_all(rec)
        dep.dry = False
        emit_all(Stream(rec.order))

        for i in range(2):
            if Sx[i].count > 0:
                nc.gpsimd.wait_ge(Sx[i].sem, Sx[i].count)
    return nc


def _host_consts():
    ident = np.eye(128, dtype=np.float32)
    bias = np.zeros((128, 4, 4, 128), dtype=np.float32)
    s = np.arange(128)[:, None]
    q = np.arange(128)[None, :]
    for g in range(2):
        for hh in range(4):
            h = 4 * g + hh
            slope = 2.0 ** (-(h + 1))
            dist_prev = q - s + 128
            bias[:, g * 2 + 0, hh, :] = np.where(dist_prev < 128, -slope * 8.0 * dist_prev, -30000.0)
            dist_cur = q - s
            bias[:, g * 2 + 1, hh, :] = np.where(dist_cur >= 0, -slope * 8.0 * dist_cur, -30000.0)
    return ident, bias.reshape(128, 2048)


def make_in_maps(x, norm1_g, w_in, conv_w, sinks, attn_norm_g, conv_norm_g,
                 w_out, norm2_g, w_ff1, w_ff2, final_g):
    f = lambda a: np.ascontiguousarray(np.asarray(a, dtype=np.float32))
    ident, bias = _host_consts()
    bc = lambda v: f(np.broadcast_to(np.asarray(v, dtype=np.float32).reshape(1, -1), (128, np.asarray(v).size)))
    common = {
        "w_in": f(w_in[0]), "w_out": f(w_out[0]), "w_ff1": f(w_ff1[0]), "w_ff2": f(w_ff2[0]),
        "g1": bc(norm1_g[0]), "g2": bc(norm2_g[0]), "gf": bc(final_g), "ga": bc(attn_norm_g[0]),
        "gc": f(np.asarray(conv_norm_g[0]).reshape(4, 128).T),
        "cw": f(np.asarray(conv_w[0]).reshape(3, 4, 128).transpose(2, 1, 0).reshape(128, 12)),
        "sk": bc(sinks[0]), "ident": ident, "bias": bias,
    }
    x = np.asarray(x, dtype=np.float32)
    maps = []
    for i in range(NCORES):
        m = dict(common)
        m["x"] = np.ascontiguousarray(x[2 * i:2 * i + 2].reshape(2 * SEQ, D))
        maps.append(m)
    return maps


def kernel(x, norm1_g, w_in, conv_w, sinks, attn_norm_g, conv_norm_g,
           w_out, norm2_g, w_ff1, w_ff2, final_g):
    maps = make_in_maps(x, norm1_g, w_in, conv_w, sinks, attn_norm_g, conv_norm_g,
                        w_out, norm2_g, w_ff1, w_ff2, final_g)
    nc = build(NT)
    res = run_bass_kernel_spmd(nc, maps, core_ids=list(range(NCORES)))
    outs = [np.asarray(r["out"], dtype=np.float32).reshape(2, SEQ, D) for r in res.results]
    return np.concatenate(outs, axis=0)
```

```python
import numpy as np
from contextlib import ExitStack
import concourse.bass as bass
import concourse.mybir as mybir
from concourse.bass_utils import run_bass_kernel_spmd

F32 = mybir.dt.float32
BF16 = mybir.dt.bfloat16
AF = mybir.ActivationFunctionType
ALU = mybir.AluOpType

NCORES = 8
D = 1024
SEQ = 4096
T = 512
NT = 16
TPS = 8
EPS = 1e-5
NS = 5
CH_Q, CH_KV, CH_V0, CH_F0, CH_G0, CH_H0 = 0, 1, 2, 6, 8, 16
NCHUNK = 24
CH_SIZE = [4096] * NCHUNK
CH_SIZE[CH_KV] = 2048
for _i in range(4):
    CH_SIZE[CH_V0 + _i] = 3072


class Tok:
    __slots__ = ("sem", "val", "eng")

    def __init__(self, sem, val, eng):
        self.sem, self.val, self.eng = sem, val, eng


class Buf:
    def __init__(self, name, t=None):
        self.name, self.t = name, t
        self.w = None
        self.r = {}


class Eng:
    def __init__(self, nc, h, name, es, is_pe=False):
        self.h = h
        self.name = name
        self.sem = es.enter_context(nc.semaphore("e_" + name))
        self.count = 0
        self.waited = {}
        self.is_pe = is_pe


class DSem:
    def __init__(self, nc, name, es):
        self.sem = es.enter_context(nc.semaphore("d_" + name))
        self.count = 0


class Dep:
    def wait(self, eng, reads, writes):
        need = {}

        def add(tok, raw):
            if tok is None:
                return
            if tok.eng is eng:
                if eng.is_pe or not raw:
                    return
            k = id(tok.sem)
            if k not in need or need[k].val < tok.val:
                need[k] = tok

        for b in reads:
            add(b.w, True)
        for b in writes:
            add(b.w, False)
            for t in b.r.values():
                add(t, False)
        for k, tok in need.items():
            if eng.waited.get(k, 0) < tok.val:
                eng.h.wait_ge(tok.sem, tok.val)
                eng.waited[k] = tok.val

    def _update(self, tok, reads, writes):
        for b in writes:
            b.w = tok
            b.r = {}
        for b in reads:
            b.r[id(tok.sem)] = tok

    dry = False

    def op(self, eng, fn, reads=(), writes=()):
        if self.dry:
            return
        self.wait(eng, reads, writes)
        inst = fn()
        eng.count += 1
        inst.then_inc(eng.sem, 1)
        self._update(Tok(eng.sem, eng.count, eng), reads, writes)

    def group(self, eng, fns, reads=(), writes=()):
        if self.dry:
            return
        self.wait(eng, reads, writes)
        inst = None
        for fn in fns:
            inst = fn()
        eng.count += 1
        inst.then_inc(eng.sem, 1)
        self._update(Tok(eng.sem, eng.count, eng), reads, writes)

    def dma(self, qeng, fns, dsem, reads=(), writes=()):
        if self.dry:
            return
        self.wait(qeng, reads, writes)
        for fn in fns:
            fn().then_inc(dsem.sem, 16)
            dsem.count += 16
        self._update(Tok(dsem.sem, dsem.count, None), reads, writes)


def build(nt=NT, do_prologue=True):
    nc = bass.Bass("TRN2", target_bir_lowering=False)

    def din(name, shape):
        return nc.dram_tensor(name, shape, F32, kind="ExternalInput").ap()

    x_d = din("x", [2 * SEQ, D])
    w_in_d = din("w_in", [D, 2304])
    w_out_d = din("w_out", [D, D])
    w_ff1_d = din("w_ff1", [D, 4096])
    w_ff2_d = din("w_ff2", [4096, D])
    g1_d = din("g1", [128, D])
    g2_d = din("g2", [128, D])
    gf_d = din("gf", [128, D])
    ga_d = din("ga", [128, 512])
    gc_d = din("gc", [128, 4])
    cw_d = din("cw", [128, 12])
    sk_d = din("sk", [128, 8])
    id_d = din("ident", [128, 128])
    bias_d = din("bias", [128, 2048])
    out_d = nc.dram_tensor("out", [2 * SEQ, D], F32, kind="ExternalOutput").ap()
    wscr = nc.dram_tensor("wscr", [NCHUNK, 128, 4096], BF16).ap()

    dep = Dep()
    with ExitStack() as es:
        def sb(name, shape, dt):
            return Buf(name, es.enter_context(nc.sbuf_tensor("s_" + name, shape, dt)))

        def ps(name, shape, dt):
            return Buf(name, es.enter_context(nc.psum_tensor("p_" + name, shape, dt)))

        PE = Eng(nc, nc.tensor, "pe", es, is_pe=True)
        ACT = Eng(nc, nc.scalar, "act", es)
        DVE = Eng(nc, nc.vector, "dve", es)
        POOL = Eng(nc, nc.gpsimd, "pool", es)
        SP = Eng(nc, nc.sync, "sp", es)

        X = [sb(f"X{i}", [128, 4, D], F32) for i in range(2)]
        junk = es.enter_context(nc.sbuf_tensor("junk", [128, D], BF16))
        xsB = sb("xsB", [128, D], BF16)
        xsA = sb("xsA", [128, D], BF16)
        xT = sb("xT", [128, 8, T], BF16)
        zT = sb("zT", [128, 8, T], BF16)
        qz = [sb(f"qz{i}", [128, 4, T], BF16) for i in range(2)]
        kT = [sb(f"kT{i}", [128, T], BF16) for i in range(2)]
        vtok = [sb(f"vtok{i}", [128, 4, 2, 65], BF16) for i in range(2)]
        u_sb = sb("u_sb", [128, T], F32)
        cu = sb("cu", [128, 4, T + 2], F32)
        conv_f = sb("conv_f", [128, 4, T], F32)
        sq = sb("sq", [128, T], BF16)
        invc = sb("invc", [128, T], F32)
        mixedT = sb("mixedT", [128, 8, T], BF16)
        PT = sb("PT", [128, 4, 512], BF16)
        xin = [sb(f"xin{i}", [128, D], F32) for i in range(2)]
        attn_n = sb("attn_n", [128, 512], F32)
        attn_s = sb("attn_s", [128, 512], BF16)
        aT = sb("aT", [128, 32, T], BF16)
        relu_t = [sb(f"relu{i}", [128, T], F32) for i in range(2)]
        R = [sb(f"ring{i}", [128, 4096], BF16) for i in range(NS)]
        ident = sb("ident", [128, 128], BF16)
        ones_bf = sb("ones_bf", [128, 128], BF16)
        biasT = sb("biasT", [128, 4, 512], BF16)
        g1 = sb("g1", [128, D], F32)
        g2 = sb("g2", [128, D], F32)
        gf = sb("gf", [128, D], F32)
        ga = sb("ga", [128, 512], F32)
        gc = sb("gc", [128, 4], F32)
        cw = sb("cw", [128, 12], F32)
        es_bc = sb("es_bc", [128, 8], F32)
        epsc = sb("epsc", [128, 1], F32)
        stats = {}
        for key in ("A0", "B0", "B1", "Bq", "Af"):
            stats[key] = [sb(f"st_{key}{i}", [128, 4], F32) for i in range(3)]
        den = sb("den", [128, 8], F32)
        rden = sb("rden", [128, 8], F32)

        FA = [ps(f"FA{i}", [128, 512], F32) for i in range(4)]
        PB = [ps(f"PB{i}", [128, 512], F32) for i in range(4)]

        def tpview(bank):
            return bank.t[:, :].bitcast(BF16).rearrange("p (k j) -> p k j", j=128)

        Lx = [DSem(nc, f"lx{i}", es) for i in range(2)]
        Lin = [DSem(nc, f"lin{i}", es) for i in range(2)]
        Sx = [DSem(nc, f"sx{i}", es) for i in range(2)]
        RS = [DSem(nc, f"rs{i}", es) for i in range(NS)]
        RST = [DSem(nc, f"rst{i}", es) for i in range(NS)]
        CS = DSem(nc, "cs", es)
        WS = [Buf(f"ws{c}") for c in range(NCHUNK)]

        cstage = X[1]
        cst = cstage.t[:].rearrange("p a b -> p (a b)")
        dep.dma(SP, [
            lambda: nc.sync.dma_start(out=cst[:, 0:128], in_=id_d[:, :]),
            lambda: nc.sync.dma_start(out=cst[:, 128:128 + 2048], in_=bias_d[:, :]),
        ], Lx[1], writes=[cstage])
        dep.dma(SP, [
            lambda: nc.sync.dma_start(out=g1.t[:, :], in_=g1_d[:, :]),
            lambda: nc.sync.dma_start(out=g2.t[:, :], in_=g2_d[:, :]),
            lambda: nc.sync.dma_start(out=gf.t[:, :], in_=gf_d[:, :]),
            lambda: nc.sync.dma_start(out=ga.t[:, :], in_=ga_d[:, :]),
            lambda: nc.sync.dma_start(out=gc.t[:, :], in_=gc_d[:, :]),
            lambda: nc.sync.dma_start(out=cw.t[:, :], in_=cw_d[:, :]),
            lambda: nc.sync.dma_start(out=es_bc.t[:, :], in_=sk_d[:, :]),
        ], CS, writes=[g1, g2, gf, ga, gc, cw, es_bc])
        dep.op(DVE, lambda: nc.vector.tensor_copy(out=ident.t[:, :], in_=cst[:, 0:128]),
               reads=[cstage], writes=[ident])
        dep.op(DVE, lambda: nc.vector.tensor_copy(
            out=biasT.t[:].rearrange("p a b -> p (a b)"), in_=cst[:, 128:128 + 2048]),
            reads=[cstage], writes=[biasT])
        dep.op(DVE, lambda: nc.vector.memset(ones_bf.t[:, :], 1.0), writes=[ones_bf])
        dep.op(DVE, lambda: nc.vector.memset(epsc.t[:, :], EPS), writes=[epsc])
        for i in range(2):
            dep.op(DVE, lambda i=i: nc.vector.memset(
                vtok[i].t[:].rearrange("p a b c -> p (a b c)"), 1.0), writes=[vtok[i]])
            dep.op(DVE, lambda i=i: nc.vector.memset(
                qz[i].t[:].rearrange("p a b -> p (a b)"), 0.0), writes=[qz[i]])
        dep.op(ACT, lambda: nc.scalar.activation(out=es_bc.t[:, :], in_=es_bc.t[:, :], func=AF.Exp),
               reads=[es_bc], writes=[es_bc])

        def src_views(c):
            res = []
            wv_in = w_in_d.rearrange("(k p) c -> p k c", p=128)
            if c == CH_Q:
                for j in range(4):
                    for half in range(2):
                        col = (half * 4 + j) * 64
                        res.append((lambda s, j=j, half=half: s.rearrange("p (k c) -> p k c", c=512)[
                            :, :, j * 128 + half * 64: j * 128 + half * 64 + 64],
                            wv_in[:, :, col:col + 64]))
            elif c == CH_KV:
                res.append((lambda s: s[:, 0:2048].rearrange("p (k c) -> p k c", c=256),
                            wv_in[:, :, 512:768]))
            elif CH_V0 <= c < CH_V0 + 4:
                ct = c - CH_V0
                for ui, c0 in enumerate((1792, 768, 1280)):
                    res.append((lambda s, ui=ui: s[:, 0:3072].rearrange("p (k c) -> p k c", c=384)[
                        :, :, ui * 128:(ui + 1) * 128],
                        wv_in[:, :, c0 + ct * 128:c0 + (ct + 1) * 128]))
            elif c in (CH_F0, CH_F0 + 1):
                dh = c - CH_F0
                wv = w_out_d.rearrange("(e p) d -> p e d", p=128)
                res.append((lambda s: s.rearrange("p (e d) -> p e d", d=512), wv[:, :, dh * 512:(dh + 1) * 512]))
            elif CH_G0 <= c < CH_H0:
                j = c - CH_G0
                wv = w_ff1_d.rearrange("(k p) f -> p k f", p=128)
                res.append((lambda s: s.rearrange("p (k c) -> p k c", c=512), wv[:, :, j * 512:(j + 1) * 512]))
            else:
                j = c - CH_H0
                dh, fg = j // 4, j % 4
                wv = w_ff2_d.rearrange("(f p) d -> p f d", p=128)
                res.append((lambda s: s.rearrange("p (k c) -> p k c", c=512),
                            wv[:, fg * 8:(fg + 1) * 8, dh * 512:(dh + 1) * 512]))
            return res

        PCS = [DSem(nc, f"pc{c}", es) for c in range(NCHUNK)]

        pro_state = {"next": 0}

        def ensure_prologue(cmax):
            if not do_prologue:
                return
            while pro_state["next"] <= cmax:
                c = pro_state["next"]
                pro_state["next"] += 1
                n = CH_SIZE[c]
                fns = []
                for (dv, src) in src_views(c):
                    fns.append(lambda dv=dv, src=src, c=c, n=n: nc.gpsimd.dma_start(
                        out=dv(wscr[c, :, 0:n]) if n == 4096 else dv(wscr[c]), in_=src))
                dep.dma(POOL, fns, PCS[c], writes=[WS[c]])

        class Stream:
            def __init__(self, order=None):
                self.record = order is None
                self.order = [] if order is None else order
                self.acq = 0
                self.issued = 0
                self.free = list(range(NS))
                self.slot_of = {}
                if not self.record:
                    self.pump(limit=2)

            def pump(self, limit=None):
                while self.issued < len(self.order) and self.free and (limit is None or self.issued < limit):
                    n = self.issued
                    c = self.order[n]
                    slot = self.free.pop(0)
                    self.slot_of[n] = slot
                    sz = CH_SIZE[c]
                    ensure_prologue(c)
                    dep.dma(SP, [lambda: nc.sync.dma_start(out=R[slot].t[:, 0:sz], in_=wscr[c, :, 0:sz])],
                            RS[slot], reads=[WS[c]], writes=[R[slot]])
                    self.issued += 1

            def acquire(self, c):
                n = self.acq
                self.acq += 1
                if self.record:
                    self.order.append(c)
                    return n, R[0]
                self.pump()
                assert self.order[n] == c, (n, c, self.order[n])
                assert n < self.issued, ("ring too small / order deadlock", n, self.issued)
                return n, R[self.slot_of[n]]

            def release(self, n):
                if self.record:
                    return
                self.free.append(self.slot_of[n])
                self.pump()

        def emit_all(stream):
            BLOCKED = ("blocked",)
            flags = {"mixer_done": set(), "store_done": set(), "x_loaded": set()}

            def load_xin(t, g):
                seq, tb = t // TPS, t % TPS
                r0 = seq * SEQ + tb * T + g * 128
                b = xin[g % 2]
                dep.dma(POOL, [lambda: nc.gpsimd.dma_start(out=b.t[:, :], in_=x_d[r0:r0 + 128, :])],
                        Lin[g % 2], writes=[b])

            def load_x(t):
                seq, tb = t // TPS, t % TPS
                r0 = seq * SEQ + tb * T
                xb = X[t % 2]
                dep.dma(POOL, [lambda: nc.gpsimd.dma_start(
                    out=xb.t[:, :, :], in_=x_d[r0:r0 + T, :].rearrange("(g p) d -> p g d", p=128))],
                    Lx[t % 2], writes=[xb])
                flags["x_loaded"].add(t)

            def try_load_x(t):
                if t in flags["x_loaded"]:
                    return True
                if t >= 2 and (t - 2) not in flags["store_done"]:
                    return False
                load_x(t)
                return True

            def store_x(t):
                seq, tb = t // TPS, t % TPS
                r0 = seq * SEQ + tb * T
                xb = X[t % 2]
                dep.dma(POOL, [lambda: nc.gpsimd.dma_start(
                    out=out_d[r0:r0 + T, :].rearrange("(g p) d -> p g d", p=128), in_=xb.t[:, :, :])],
                    Sx[t % 2], reads=[xb])
                flags["store_done"].add(t)

            def rms_inv_batch(srcs, n_feat, key):
                ss, ln_, inv = stats[key]
                nb_ = len(srcs)
                for i, (sap, sbuf) in enumerate(srcs):
                    dep.op(ACT, lambda: nc.scalar.activation(
                        out=junk[:, 0:n_feat], in_=sap, func=AF.Square, accum_out=ss.t[:, i:i + 1]),
                        reads=[sbuf], writes=[ss])
                dep.op(ACT, lambda: nc.scalar.activation(
                    out=ln_.t[:, 0:nb_], in_=ss.t[:, 0:nb_], func=AF.Ln, scale=1.0 / n_feat, bias=epsc.t[:, 0:1]),
                    reads=[ss, epsc], writes=[ln_])
                dep.op(ACT, lambda: nc.scalar.activation(
                    out=inv.t[:, 0:nb_], in_=ln_.t[:, 0:nb_], func=AF.Exp, scale=-0.5),
                    reads=[ln_], writes=[inv])
                return inv

            def norm_transpose(src, gtab, dstT, xsb, bank, th, batch, after_stt=None):
                tpv = tpview(bank)
                for bi, g0 in enumerate(range(0, 4, batch)):
                    gs = list(range(g0, g0 + batch))
                    inv = rms_inv_batch([src(g) for g in gs], D, th + str(bi))
                    yield (1.2 * batch + 0.6, 1.0 * batch + 1.0)
                    for i, g in enumerate(gs):
                        sap, sbuf = src(g)
                        dep.op(DVE, lambda: nc.vector.scalar_tensor_tensor(
                            out=xsb.t[:, :], in0=sap, scalar=inv.t[:, i:i + 1], in1=gtab.t[:, :],
                            op0=ALU.mult, op1=ALU.mult), reads=[sbuf, inv, gtab], writes=[xsb])
                        if after_stt is not None:
                            after_stt(g)
                        yield (1.3, 1.5)
                        dep.group(PE, [lambda k=k: nc.tensor.transpose(
                            out=tpv[:, k, :], in_=xsb.t[:, k * 128:(k + 1) * 128], identity=ident.t[:, :])
                            for k in range(8)], reads=[xsb, ident], writes=[bank])
                        dep.op(ACT, lambda: nc.scalar.activation(
                            out=dstT.t[:, :, g * 128:(g + 1) * 128], in_=tpv[:, :, :], func=AF.Copy),
                            reads=[bank], writes=[dstT])
                        yield (1.7, 1.2 if g == 3 else 0)

            def proj_unit(bank, wslot, wview, col0, srcT):
                dep.group(PE, [lambda k=k: nc.tensor.matmul(
                    bank.t[:, :], lhsT=wview[:, k, col0:col0 + 128], rhs=srcT.t[:, k, :],
                    start=(k == 0), stop=(k == 7)) for k in range(8)],
                    reads=[wslot, srcT], writes=[bank])

            def mixer(t):
                seq, tb = t // TPS, t % TPS
                xb = X[t % 2]
                par = t % 2
                try_load_x(t)

                def after_stt(g):
                    if g + 2 < 4:
                        load_xin(t, g + 2)
                    elif t + 1 < nt:
                        load_xin(t + 1, g - 2)

                yield from norm_transpose(lambda g: (xin[g % 2].t[:, :], xin[g % 2]), g1, xT, xsB, PB[0], "B", 2,
                                          after_stt)
                n, Qc = stream.acquire(CH_Q)
                Qv = Qc.t[:].rearrange("p (k c) -> p k c", c=512)
                for j in range(4):
                    bank = PB[1 + (j % 2)]
                    proj_unit(bank, Qc, Qv, j * 128, xT)
                    dep.op(ACT, lambda: nc.scalar.activation(
                        out=qz[0].t[0:64, j, :], in_=bank.t[0:64, :], func=AF.Copy),
                        reads=[bank], writes=[qz[0]])
                    dep.op(DVE, lambda: nc.vector.tensor_copy(
                        out=qz[1].t[64:128, j, :], in_=bank.t[64:128, :]), reads=[bank], writes=[qz[1]])
                    yield (2.0, 0)
                stream.release(n)
                n, KVc = stream.acquire(CH_KV)
                KVv = KVc.t[:, 0:2048].rearrange("p (k c) -> p k c", c=256)
                bank = PB[1]
                proj_unit(bank, KVc, KVv, 0, xT)
                dep.op(DVE, lambda: nc.vector.tensor_copy(out=kT[par].t[:, :], in_=bank.t[:, :]),
                       reads=[bank], writes=[kT[par]])
                yield (2.0, 0)
                bank = PB[2]
                fns = []
                for tg in range(4):
                    for k in range(8):
                        fns.append(lambda tg=tg, k=k: nc.tensor.matmul(
                            bank.t[:, tg * 128:(tg + 1) * 128], lhsT=xT.t[:, k, tg * 128:(tg + 1) * 128],
                            rhs=KVv[:, k, 128:256], start=(k == 0), stop=(k == 7)))
                dep.group(PE, fns, reads=[KVc, xT], writes=[bank])
                dep.op(DVE, lambda: nc.vector.tensor_copy(
                    out=vtok[par].t[:, :, :, 0:64],
                    in_=bank.t[:, :].rearrange("p (a b c) -> p a b c", a=4, b=2)),
                    reads=[bank], writes=[vtok[par]])
                stream.release(n)
                yield (3.0, 0)
                if tb == 0:
                    dep.op(DVE, lambda: nc.vector.memset(cu.t[:, :, 0:2], 0.0), writes=[cu])
                ssc = PB[3]
                for ct in range(4):
                    n, Vc = stream.acquire(CH_V0 + ct)
                    Vv = Vc.t[:, 0:3072].rearrange("p (k c) -> p k c", c=384)
                    bank = PB[1]
                    proj_unit(bank, Vc, Vv, 0, xT)
                    dep.op(ACT, lambda: nc.scalar.activation(out=u_sb.t[:, :], in_=bank.t[:, :], func=AF.Copy),
                           reads=[bank], writes=[u_sb])
                    yield (2.0, 0)
                    bank = PB[2]
                    proj_unit(bank, Vc, Vv, 128, xT)
                    dep.op(DVE, lambda: nc.vector.tensor_tensor(
                        out=cu.t[:, ct, 2:T + 2], in0=bank.t[:, :], in1=u_sb.t[:, :], op=ALU.mult),
                        reads=[bank, u_sb], writes=[cu])
                    y = conv_f.t[:, ct, :]
                    dep.op(DVE, lambda: nc.vector.tensor_scalar(
                        out=y, in0=cu.t[:, ct, 2:T + 2], scalar1=cw.t[:, ct * 3 + 2:ct * 3 + 3], scalar2=None,
                        op0=ALU.mult), reads=[cu, cw], writes=[conv_f])
                    dep.op(DVE, lambda: nc.vector.scalar_tensor_tensor(
                        out=y, in0=cu.t[:, ct, 1:T + 1], scalar=cw.t[:, ct * 3 + 1:ct * 3 + 2], in1=y,
                        op0=ALU.mult, op1=ALU.add), reads=[cu, cw, conv_f], writes=[conv_f])
                    dep.op(DVE, lambda: nc.vector.scalar_tensor_tensor(
                        out=y, in0=cu.t[:, ct, 0:T], scalar=cw.t[:, ct * 3:ct * 3 + 1], in1=y,
                        op0=ALU.mult, op1=ALU.add), reads=[cu, cw, conv_f], writes=[conv_f])
                    yield (4.0, 0)
                    bank = PB[1]
                    proj_unit(bank, Vc, Vv, 256, xT)
                    stream.release(n)
                    dep.op(DVE, lambda: nc.vector.tensor_tensor(
                        out=y, in0=bank.t[:, :], in1=y, op=ALU.mult), reads=[bank, conv_f], writes=[conv_f])
                    dep.op(ACT, lambda: nc.scalar.activation(out=sq.t[:, :], in_=y, func=AF.Square),
                           reads=[conv_f], writes=[sq])
                    yield (3.3, 1.5)
                    dep.group(PE, [lambda: nc.tensor.matmul(
                        ssc.t[:, :], lhsT=ones_bf.t[:, :], rhs=sq.t[:, :], start=(ct == 0), stop=(ct == 3))],
                        reads=[sq, ones_bf], writes=[ssc])
                    yield (0.3, 0)
                if tb != TPS - 1:
                    dep.op(DVE, lambda: nc.vector.tensor_copy(out=cu.t[:, :, 0:2], in_=cu.t[:, :, T:T + 2]),
                           reads=[cu], writes=[cu])
                dep.op(ACT, lambda: nc.scalar.activation(
                    out=invc.t[:, :], in_=ssc.t[:, :], func=AF.Ln, scale=1.0 / 512, bias=epsc.t[:, 0:1]),
                    reads=[ssc, epsc], writes=[invc])
                dep.op(ACT, lambda: nc.scalar.activation(
                    out=invc.t[:, :], in_=invc.t[:, :], func=AF.Exp, scale=-0.5), reads=[invc], writes=[invc])
                for ct in range(4):
                    dep.op(DVE, lambda: nc.vector.scalar_tensor_tensor(
                        out=mixedT.t[:, 4 + ct, :], in0=conv_f.t[:, ct, :], scalar=gc.t[:, ct:ct + 1],
                        in1=invc.t[:, :], op0=ALU.mult, op1=ALU.mult),
                        reads=[conv_f, gc, invc], writes=[mixedT])
                try_load_x(t)
                yield (2.0, 0)

                for qb in range(4):
                    nb = tb * 4 + qb
                    blks = [1] if nb == 0 else [0, 1]
                    pt = PT
                    for g in range(2):
                        for blk in blks:
                            if blk == 1:
                                ksrc, kc0 = kT[par], qb * 128
                            elif qb > 0:
                                ksrc, kc0 = kT[par], (qb - 1) * 128
                            else:
                                ksrc, kc0 = kT[1 - par], 384
                            sc = PB[1 + blk]
                            dep.group(PE, [
                                lambda: nc.tensor.matmul(
                                    sc.t[:, :], lhsT=ksrc.t[:, kc0:kc0 + 128],
                                    rhs=qz[g].t[:, :, qb * 128:(qb + 1) * 128], start=True, stop=False),
                                lambda: nc.tensor.matmul(
                                    sc.t[:, :], lhsT=ident.t[:, :], rhs=biasT.t[:, g * 2 + blk, :],
                                    start=False, stop=True),
                            ], reads=[ksrc, qz[g], ident, biasT], writes=[sc])
                            dep.op(ACT, lambda: nc.scalar.activation(
                                out=pt.t[:, g * 2 + blk, :], in_=sc.t[:, :], func=AF.Exp, scale=0.125),
                                reads=[sc], writes=[pt])
                        yield (1.7, 1.5 if g == 1 else 0)
                    pvo = [PB[3], PB[0]]
                    for g in range(2):
                        fns = []
                        rbufs = [pt]
                        for hh in range(4):
                            for bi, blk in enumerate(blks):
                                if blk == 1:
                                    vsrc, vb = vtok[par], qb
                                elif qb > 0:
                                    vsrc, vb = vtok[par], qb - 1
                                else:
                                    vsrc, vb = vtok[1 - par], 3
                                if vsrc not in rbufs:
                                    rbufs.append(vsrc)
                                fns.append(lambda hh=hh, blk=blk, vsrc=vsrc, vb=vb, bi=bi: nc.tensor.matmul(
                                    pvo[g].t[:, hh * 65:(hh + 1) * 65],
                                    lhsT=pt.t[:, g * 2 + blk, hh * 128:(hh + 1) * 128],
                                    rhs=vsrc.t[:, vb, g, :], start=(bi == 0), stop=(bi == len(blks) - 1)))
                        dep.group(PE, fns, reads=rbufs, writes=[pvo[g]])
                        pv3 = pvo[g].t[:, 0:260].rearrange("p (h d) -> p h d", d=65)
                        dep.op(DVE, lambda: nc.vector.tensor_tensor(
                            out=den.t[:, g * 4:(g + 1) * 4], in0=pv3[:, :, 64], in1=es_bc.t[:, g * 4:(g + 1) * 4],
                            op=ALU.add), reads=[pvo[g], es_bc], writes=[den])
                        yield (0.8, 0)
                    dep.op(DVE, lambda: nc.vector.reciprocal(out=rden.t[:, :], in_=den.t[:, :]),
                           reads=[den], writes=[rden])
                    for g in range(2):
                        pv3 = pvo[g].t[:, 0:260].rearrange("p (h d) -> p h d", d=65)
                        dep.op(DVE, lambda: nc.vector.tensor_tensor(
                            out=attn_n.t[:, g * 256:(g + 1) * 256].rearrange("p (h d) -> p h d", d=64),
                            in0=pv3[:, :, 0:64],
                            in1=rden.t[:, g * 4:(g + 1) * 4].unsqueeze(2).to_broadcast([128, 4, 64]),
                            op=ALU.mult), reads=[pvo[g], rden], writes=[attn_n])
                    inv = rms_inv_batch([(attn_n.t[:, :], attn_n)], 512, "Bq")
                    dep.op(DVE, lambda: nc.vector.scalar_tensor_tensor(
                        out=attn_s.t[:, :], in0=attn_n.t[:, :], scalar=inv.t[:, 0:1], in1=ga.t[:, :],
                        op0=ALU.mult, op1=ALU.mult), reads=[attn_n, inv, ga], writes=[attn_s])
                    yield (3.2, 3.5)
                    tpb = PB[0]
                    tpv = tpview(tpb)
                    dep.group(PE, [lambda e=e: nc.tensor.transpose(
                        out=tpv[:, e, :], in_=attn_s.t[:, e * 128:(e + 1) * 128], identity=ident.t[:, :])
                        for e in range(4)], reads=[attn_s, ident], writes=[tpb])
                    dep.op(ACT, lambda: nc.scalar.activation(
                        out=mixedT.t[:, 0:4, qb * 128:(qb + 1) * 128], in_=tpv[:, 0:4, :], func=AF.Copy),
                        reads=[tpb], writes=[mixedT])
                    yield (1.3, 0)

                while not try_load_x(t):
                    yield BLOCKED
                for dh in range(2):
                    n, Fc = stream.acquire(CH_F0 + dh)
                    Fv = Fc.t[:].rearrange("p (e d) -> p e d", d=512)
                    for tg in range(4):
                        bank = PB[1 + (tg % 2)]
                        dep.group(PE, [lambda e=e: nc.tensor.matmul(
                            bank.t[:, :], lhsT=mixedT.t[:, e, tg * 128:(tg + 1) * 128],
                            rhs=Fv[:, e, :], start=(e == 0), stop=(e == 7))
                            for e in range(8)], reads=[mixedT, Fc], writes=[bank])
                        dep.op(DVE, lambda: nc.vector.tensor_tensor(
                            out=xb.t[:, tg, dh * 512:(dh + 1) * 512], in0=bank.t[:, :],
                            in1=xb.t[:, tg, dh * 512:(dh + 1) * 512], op=ALU.add), reads=[bank, xb], writes=[xb])
                        yield (2.0, 0)
                    stream.release(n)
                flags["mixer_done"].add(t)

            def ffn(t):
                xb = X[t % 2]
                while t not in flags["mixer_done"]:
                    yield BLOCKED
                yield from norm_transpose(lambda g: (xb.t[:, g, :], xb), g2, zT, xsA, FA[3], "A", 4)
                for c in range(8):
                    n, G = stream.acquire(CH_G0 + c)
                    Gv = G.t[:].rearrange("p (k c) -> p k c", c=512)
                    for fi in range(4):
                        ft = c * 4 + fi
                        bank = FA[ft % 2]
                        proj_unit(bank, G, Gv, fi * 128, zT)
                        rl = relu_t[ft % 2]
                        dep.op(ACT, lambda: nc.scalar.activation(out=rl.t[:, :], in_=bank.t[:, :], func=AF.Relu),
                               reads=[bank], writes=[rl])
                        dep.op(DVE, lambda: nc.vector.tensor_tensor(
                            out=aT.t[:, ft, :], in0=bank.t[:, :], in1=rl.t[:, :], op=ALU.mult),
                            reads=[bank, rl], writes=[aT])
                        yield (2.0, 1.5 if ft == 31 else 0)
                    stream.release(n)
                for dh in range(2):
                    for fg in range(4):
                        n, H = stream.acquire(CH_H0 + dh * 4 + fg)
                        Hv = H.t[:].rearrange("p (k c) -> p k c", c=512)
                        for fp in range(4):
                            fns = []
                            for fi in (2 * fp, 2 * fp + 1):
                                ft = fg * 8 + fi
                                for tg in range(4):
                                    fns.append(lambda fi=fi, ft=ft, tg=tg: nc.tensor.matmul(
                                        FA[tg].t[:, :], lhsT=aT.t[:, ft, tg * 128:(tg + 1) * 128],
                                        rhs=Hv[:, fi, :], start=(ft == 0), stop=(ft == 31)))
                            dep.group(PE, fns, reads=[aT, H], writes=[FA[0], FA[1], FA[2], FA[3]])
                            yield (2.0, 0)
                        stream.release(n)
                    for tg in range(4):
                        dep.op(DVE, lambda: nc.vector.tensor_tensor(
                            out=xb.t[:, tg, dh * 512:(dh + 1) * 512], in0=FA[tg].t[:, :],
                            in1=xb.t[:, tg, dh * 512:(dh + 1) * 512], op=ALU.add),
                            reads=[FA[tg], xb], writes=[xb])
                    yield (2.8, 0)
                inv = rms_inv_batch([(xb.t[:, g, :], xb) for g in range(4)], D, "Af")
                yield (5.4, 0)
                for g in range(4):
                    dep.op(DVE, lambda: nc.vector.scalar_tensor_tensor(
                        out=xb.t[:, g, :], in0=xb.t[:, g, :], scalar=inv.t[:, g:g + 1], in1=gf.t[:, :],
                        op0=ALU.mult, op1=ALU.mult), reads=[xb, inv, gf], writes=[xb])
                    yield (1.3, 0)
                store_x(t)

            TOT_A, TOT_B, LEAD = 161.6, 124.0, 1.3

            def chain(fn):
                for t in range(nt):
                    yield from fn(t)

            def schedule():
                gens = {"A": chain(ffn), "B": chain(mixer)}
                alive = {"A": nt > 0, "B": nt > 0}
                prog = {"A": 0.0, "B": 0.0}
                owed = {"A": 0.0, "B": 0.0}
                other = {"A": "B", "B": "A"}
                avoid = None
                blocked_streak = 0
                while alive["A"] or alive["B"]:
                    if not alive["B"]:
                        pick = "A"
                    elif not alive["A"]:
                        pick = "B"
                    elif avoid is not None:
                        pick = other[avoid]
                    else:
                        pick = "B" if (prog["B"] / TOT_B - prog["A"] / TOT_A) < LEAD else "A"
                        if owed[pick] > 0:
                            o = other[pick]
                            if owed[o] > 0:
                                pick = pick if owed[pick] <= owed[o] else o
                                owed[pick] = 0.0
                            else:
                                pick = o
                    avoid = None
                    try:
                        r = next(gens[pick])
                    except StopIteration:
                        alive[pick] = False
                        owed[other[pick]] = 0.0
                        continue
                    if r is BLOCKED:
                        blocked_streak += 1
                        assert blocked_streak < 4, "scheduler deadlock"
                        assert alive[other[pick]], "blocked with no other thread"
                        avoid = pick
                        owed[other[pick]] = 0.0
                        continue
                    blocked_streak = 0
                    cost, need = r
                    prog[pick] += cost
                    owed[other[pick]] = max(0.0, owed[other[pick]] - cost)
                    owed[pick] = float(need)

            if nt > 0:
                load_xin(0, 0)
                load_xin(0, 1)
            schedule()

        dep.dry = True
        rec = Stream(None)
        emit_all(rec)
        dep.dry = False
        emit_all(Stream(rec.order))

        for i in range(2):
            if Sx[i].count > 0:
                nc.gpsimd.wait_ge(Sx[i].sem, Sx[i].count)
    return nc


def _host_consts():
    ident = np.eye(128, dtype=np.float32)
    bias = np.zeros((128, 4, 4, 128), dtype=np.float32)
    s = np.arange(128)[:, None]
    q = np.arange(128)[None, :]
    for g in range(2):
        for hh in range(4):
            h = 4 * g + hh
            slope = 2.0 ** (-(h + 1))
            dist_prev = q - s + 128
            bias[:, g * 2 + 0, hh, :] = np.where(dist_prev < 128, -slope * 8.0 * dist_prev, -30000.0)
            dist_cur = q - s
            bias[:, g * 2 + 1, hh, :] = np.where(dist_cur >= 0, -slope * 8.0 * dist_cur, -30000.0)
    return ident, bias.reshape(128, 2048)


def make_in_maps(x, norm1_g, w_in, conv_w, sinks, attn_norm_g, conv_norm_g,
                 w_out, norm2_g, w_ff1, w_ff2, final_g):
    f = lambda a: np.ascontiguousarray(np.asarray(a, dtype=np.float32))
    ident, bias = _host_consts()
    bc = lambda v: f(np.broadcast_to(np.asarray(v, dtype=np.float32).reshape(1, -1), (128, np.asarray(v).size)))
    common = {
        "w_in": f(w_in[0]), "w_out": f(w_out[0]), "w_ff1": f(w_ff1[0]), "w_ff2": f(w_ff2[0]),
        "g1": bc(norm1_g[0]), "g2": bc(norm2_g[0]), "gf": bc(final_g), "ga": bc(attn_norm_g[0]),
        "gc": f(np.asarray(conv_norm_g[0]).reshape(4, 128).T),
        "cw": f(np.asarray(conv_w[0]).reshape(3, 4, 128).transpose(2, 1, 0).reshape(128, 12)),
        "sk": bc(sinks[0]), "ident": ident, "bias": bias,
    }
    x = np.asarray(x, dtype=np.float32)
    maps = []
    for i in range(NCORES):
        m = dict(common)
        m["x"] = np.ascontiguousarray(x[2 * i:2 * i + 2].reshape(2 * SEQ, D))
        maps.append(m)
    return maps


def kernel(x, norm1_g, w_in, conv_w, sinks, attn_norm_g, conv_norm_g,
           w_out, norm2_g, w_ff1, w_ff2, final_g):
    maps = make_in_maps(x, norm1_g, w_in, conv_w, sinks, attn_norm_g, conv_norm_g,
                        w_out, norm2_g, w_ff1, w_ff2, final_g)
    nc = build(NT)
    res = run_bass_kernel_spmd(nc, maps, core_ids=list(range(NCORES)))
    outs = [np.asarray(r["out"], dtype=np.float32).reshape(2, SEQ, D) for r in res.results]
    return np.concatenate(outs, axis=0)
```
